# Optimizing a Trainium2 kernel written in Bass

```python
import jax, jax.numpy as jnp
from jax import lax
import numpy as np

D_MODEL = 1024
BATCH = 32
SEQ = 256
DEPTH = 1
DEC_BATCH = 8
DEC_SEQ = 2048
PAST_LEN = 256

GRID_W = 64
HEAD_DIM = 64
N_HEADS_A = 8
N_KV_A = 2
N_HEADS_B = 8
N_KV_B = 2
WIDTH_A = N_HEADS_A * HEAD_DIM
WIDTH_B = N_HEADS_B * HEAD_DIM
MIX_WIDTH = WIDTH_A + WIDTH_B
KV_W_A = N_KV_A * HEAD_DIM
KV_W_B = N_KV_B * HEAD_DIM
SPLIT_SIZES = (WIDTH_A, KV_W_A, KV_W_A, WIDTH_A, WIDTH_B, KV_W_B, KV_W_B, WIDTH_B)
SPLIT_IDX = tuple(int(v) for v in np.cumsum(SPLIT_SIZES)[:-1])
IN_WIDTH = int(sum(SPLIT_SIZES))
Q_BLOCK = 128
WINDOW = 128
ROPE_THETA = 10000.0
EPS = 1e-6
NEG_INF = -1e30

kernel_name = "hybrid_parallel_gqa_window_diffusion_step"


def rmsnorm(x, gain):
    xf = x.astype(jnp.float32)
    y = xf * lax.rsqrt(jnp.mean(xf * xf, axis=-1, keepdims=True) + EPS)
    return (y * gain.astype(jnp.float32)).astype(x.dtype)


def modulation(cond, w_mod, b_mod):
    m = (jax.nn.silu(cond) @ w_mod + b_mod)[..., None, :]
    shift, scale, gate = jnp.split(m, 3, axis=-1)
    return shift, scale, gate


def axial_rope(n_tokens):
    rows = n_tokens // GRID_W
    row = jnp.repeat(jnp.arange(rows, dtype=jnp.float32), GRID_W)
    col = jnp.tile(jnp.arange(GRID_W, dtype=jnp.float32), rows)
    n_freq = HEAD_DIM // 4
    inv = ROPE_THETA ** (-jnp.arange(n_freq, dtype=jnp.float32) / n_freq)
    ar = row[:, None] * inv[None, :]
    ac = col[:, None] * inv[None, :]
    ang = jnp.concatenate([ar, ar, ac, ac], axis=-1)
    return jnp.cos(ang)[:, None, :], jnp.sin(ang)[:, None, :]


def apply_rope(x, cos, sin):
    xf = x.astype(jnp.float32)
    r1, r2, c1, c2 = jnp.split(xf, 4, axis=-1)
    rot = jnp.concatenate([-r2, r1, -c2, c1], axis=-1)
    return (xf * cos + rot * sin).astype(x.dtype)


def project(h, w_in, qn_a, kn_a, qn_b, kn_b):
    b, t, _ = h.shape
    p = h @ w_in
    q_a, k_a, v_a, g_a, q_b, k_b, v_b, g_b = jnp.split(p, SPLIT_IDX, axis=-1)
    q_a = rmsnorm(q_a.reshape(b, t, N_HEADS_A, HEAD_DIM), qn_a)
    k_a = rmsnorm(k_a.reshape(b, t, N_KV_A, HEAD_DIM), kn_a)
    v_a = v_a.reshape(b, t, N_KV_A, HEAD_DIM)
    q_b = rmsnorm(q_b.reshape(b, t, N_HEADS_B, HEAD_DIM), qn_b)
    k_b = rmsnorm(k_b.reshape(b, t, N_KV_B, HEAD_DIM), kn_b)
    v_b = v_b.reshape(b, t, N_KV_B, HEAD_DIM)
    return q_a, k_a, v_a, g_a, q_b, k_b, v_b, g_b


def sink_column(sink, kv, shape_prefix):
    s = sink.astype(jnp.float32).reshape(kv, -1)[:, :, None, None]
    return jnp.broadcast_to(s, shape_prefix + (1,))


def gqa_dense(q, k, v, sink=None):
    b, t, h, d = q.shape
    kv = k.shape[2]
    qg = q.reshape(b, t, kv, h // kv, d)
    s = jnp.einsum('btkgd,bskd->bkgts', qg, k).astype(jnp.float32) * (d ** -0.5)
    if sink is not None:
        s = jnp.concatenate([sink_column(sink, kv, s.shape[:-1]), s], axis=-1)
        p = jax.nn.softmax(s, axis=-1)[..., 1:]
    else:
        p = jax.nn.softmax(s, axis=-1)
    o = jnp.einsum('bkgts,bskd->btkgd', p.astype(v.dtype), v)
    return o.reshape(b, t, h * d)


def global_attn_latent(q, k_lat, v_lat, k_ctx, v_ctx):
    b, n, h, d = q.shape
    kv = k_lat.shape[2]
    g = h // kv
    k_all = jnp.concatenate([k_ctx, k_lat], axis=1)
    v_all = jnp.concatenate([v_ctx, v_lat], axis=1)
    nb = n // Q_BLOCK
    qb = q.reshape(b, nb, Q_BLOCK, kv, g, d).transpose(1, 0, 2, 3, 4, 5)

    def one_block(qblk):
        s = jnp.einsum('bqkgd,bskd->bkgqs', qblk, k_all).astype(jnp.float32) * (d ** -0.5)
        p = jax.nn.softmax(s, axis=-1).astype(v_all.dtype)
        return jnp.einsum('bkgqs,bskd->bqkgd', p, v_all)

    o = lax.map(one_block, qb)
    return o.transpose(1, 0, 2, 3, 4, 5).reshape(b, n, h * d)


def window_attn_latent(q, k_lat, v_lat, k_ctx, v_ctx, sink):
    b, n, h, d = q.shape
    kv = k_lat.shape[2]
    g = h // kv
    nb = n // Q_BLOCK
    pad = ((0, 0), (Q_BLOCK, Q_BLOCK), (0, 0), (0, 0))
    kp = jnp.pad(k_lat, pad).reshape(b, nb + 2, Q_BLOCK, kv, d)
    vp = jnp.pad(v_lat, pad).reshape(b, nb + 2, Q_BLOCK, kv, d)
    kband = jnp.concatenate([kp[:, :-2], kp[:, 1:-1], kp[:, 2:]], axis=2)
    vband = jnp.concatenate([vp[:, :-2], vp[:, 1:-1], vp[:, 2:]], axis=2)
    qb = q.reshape(b, nb, Q_BLOCK, kv, g, d)
    scale = d ** -0.5
    s_band = jnp.einsum('bnqkgd,bnskd->bnkgqs', qb, kband).astype(jnp.float32) * scale
    qpos = jnp.arange(Q_BLOCK)[:, None]
    kpos = jnp.arange(3 * Q_BLOCK)[None, :] - Q_BLOCK
    rel = kpos - qpos
    abs_j = (jnp.arange(nb) * Q_BLOCK)[:, None, None] + kpos[None]
    valid = (jnp.abs(rel)[None] <= WINDOW) & (abs_j >= 0) & (abs_j < n)
    s_band = jnp.where(valid[None, :, None, None], s_band, NEG_INF)
    s_ctx = jnp.einsum('bnqkgd,bskd->bnkgqs', qb, k_ctx).astype(jnp.float32) * scale
    n_ctx = k_ctx.shape[1]
    s = jnp.concatenate([sink_column(sink, kv, s_ctx.shape[:-1]), s_ctx, s_band], axis=-1)
    p = jax.nn.softmax(s, axis=-1).astype(v_lat.dtype)
    p_ctx = p[..., 1:1 + n_ctx]
    p_band = p[..., 1 + n_ctx:]
    o = (jnp.einsum('bnkgqs,bskd->bnqkgd', p_ctx, v_ctx)
         + jnp.einsum('bnkgqs,bnskd->bnqkgd', p_band, vband))
    return o.reshape(b, n, h * d)


def merge_out(o_a, g_a, o_b, g_b, w_out):
    return jnp.concatenate([o_a * jax.nn.silu(g_a), o_b * jax.nn.silu(g_b)], axis=-1) @ w_out


def setup_inputs(seed: int = 0) -> dict:
    key = jax.random.key(seed)
    ks = jax.random.split(key, 20)
    f32 = jnp.float32
    nrm = lambda k, shape, s: jax.random.normal(k, shape, f32) * s
    return {
        "x_prompt": nrm(ks[0], (BATCH, SEQ, D_MODEL), 1.0),
        "x_sample": nrm(ks[1], (DEC_BATCH, DEC_SEQ, D_MODEL), 1.0),
        "cache_k_a": nrm(ks[2], (DEC_BATCH, DEPTH, PAST_LEN, N_KV_A, HEAD_DIM), 1.0),
        "cache_v_a": nrm(ks[3], (DEC_BATCH, DEPTH, PAST_LEN, N_KV_A, HEAD_DIM), 1.0),
        "cache_k_b": nrm(ks[4], (DEC_BATCH, DEPTH, PAST_LEN, N_KV_B, HEAD_DIM), 1.0),
        "cache_v_b": nrm(ks[5], (DEC_BATCH, DEPTH, PAST_LEN, N_KV_B, HEAD_DIM), 1.0),
        "c": nrm(ks[6], (DEC_BATCH, D_MODEL), 1.0),
        "c_ctx": nrm(ks[7], (D_MODEL,), 1.0),
        "w_mod": nrm(ks[8], (DEPTH, D_MODEL, 3 * D_MODEL), 0.02),
        "b_mod": nrm(ks[9], (DEPTH, 3 * D_MODEL), 0.02),
        "norm_gain": 1.0 + nrm(ks[10], (DEPTH, D_MODEL), 0.02),
        "w_in": nrm(ks[11], (DEPTH, D_MODEL, IN_WIDTH), D_MODEL ** -0.5),
        "qn_a": 1.0 + nrm(ks[12], (DEPTH, HEAD_DIM), 0.02),
        "kn_a": 1.0 + nrm(ks[13], (DEPTH, HEAD_DIM), 0.02),
        "qn_b": 1.0 + nrm(ks[14], (DEPTH, HEAD_DIM), 0.02),
        "kn_b": 1.0 + nrm(ks[15], (DEPTH, HEAD_DIM), 0.02),
        "sink_b": nrm(ks[16], (DEPTH, N_HEADS_B), 0.5),
        "w_out": nrm(ks[17], (DEPTH, MIX_WIDTH, D_MODEL), MIX_WIDTH ** -0.5),
    }


def reference(x_prompt, x_sample, cache_k_a, cache_v_a, cache_k_b, cache_v_b, c, c_ctx,
              w_mod, b_mod, norm_gain, w_in, qn_a, kn_a, qn_b, kn_b, sink_b, w_out):
    n_lat = x_sample.shape[1]
    cos, sin = axial_rope(n_lat)
    xp = x_prompt
    xs = x_sample
    new_k_a, new_v_a, new_k_b, new_v_b = [], [], [], []
    for l in range(DEPTH):
        shift, scale, gate = modulation(c_ctx, w_mod[l], b_mod[l])
        h = rmsnorm(xp, norm_gain[l]) * (1.0 + scale) + shift
        q_a, k_a, v_a, g_a, q_b, k_b, v_b, g_b = project(h, w_in[l], qn_a[l], kn_a[l], qn_b[l], kn_b[l])
        o_a = gqa_dense(q_a, k_a, v_a)
        o_b = gqa_dense(q_b, k_b, v_b, sink_b[l])
        xp = xp + gate * merge_out(o_a, g_a, o_b, g_b, w_out[l])
        new_k_a.append(k_a)
        new_v_a.append(v_a)
        new_k_b.append(k_b)
        new_v_b.append(v_b)

        shift, scale, gate = modulation(c, w_mod[l], b_mod[l])
        h = rmsnorm(xs, norm_gain[l]) * (1.0 + scale) + shift
        q_a, k_a, v_a, g_a, q_b, k_b, v_b, g_b = project(h, w_in[l], qn_a[l], kn_a[l], qn_b[l], kn_b[l])
        q_a = apply_rope(q_a, cos, sin)
        k_a = apply_rope(k_a, cos, sin)
        q_b = apply_rope(q_b, cos, sin)
        k_b = apply_rope(k_b, cos, sin)
        o_a = global_attn_latent(q_a, k_a, v_a, cache_k_a[:, l], cache_v_a[:, l])
        o_b = window_attn_latent(q_b, k_b, v_b, cache_k_b[:, l], cache_v_b[:, l], sink_b[l])
        xs = xs + gate * merge_out(o_a, g_a, o_b, g_b, w_out[l])
    nk_a = jnp.stack(new_k_a, axis=1)
    nv_a = jnp.stack(new_v_a, axis=1)
    nk_b = jnp.stack(new_k_b, axis=1)
    nv_b = jnp.stack(new_v_b, axis=1)
    return (xp, xs, nk_a, nv_a, nk_b, nv_b)
```

```python
import numpy as np
import ml_dtypes
import concourse.bass as bass
import concourse.mybir as mybir
from concourse.bass_utils import run_bass_kernel_spmd

F32 = mybir.dt.float32
BF16 = mybir.dt.bfloat16
AF = mybir.ActivationFunctionType
ALU = mybir.AluOpType

NCORES = 8
D = 1024
NS = 2048
NP = 1024
EPS = 1e-6
NEGV = -30000.0
_ENV = {}
STAGE1 = int(_ENV.get("KSTAGE", "99"))
STAGE2 = int(_ENV.get("KSTAGE2", "99"))


class Buf:
    __slots__ = ("name", "w", "r", "excl")

    def __init__(self, name, excl=False):
        self.name = name
        self.w = None
        self.r = {}
        self.excl = excl


class Op:
    __slots__ = ("eng", "fn", "deps", "pos", "signal", "semval", "dma", "dsem", "dval")

    def __init__(self, eng, fn, dma):
        self.eng = eng
        self.fn = fn
        self.dma = dma
        self.deps = []
        self.pos = -1
        self.signal = False
        self.semval = 0
        self.dsem = None
        self.dval = 0


class Prog:
    ENGS = ("pe", "act", "dve", "pool", "sp")
    SAME_ENG_WINDOW = 6

    def __init__(self, nc):
        self.nc = nc
        self.ops = {e: [] for e in self.ENGS}
        self.dma_pool_size = {"sp": 24, "pool": 12, "act": 4, "dve": 2, "pe": 2}
        self.dma_ops = {e: [] for e in self.ENGS}
        self.nbuf = 0

    def buf(self, name=None, excl=False):
        self.nbuf += 1
        return Buf(name or f"b{self.nbuf}", excl)

    def bufs(self, n, name="b"):
        return [self.buf(f"{name}{i}") for i in range(n)]

    def op(self, eng, fn, reads=(), writes=(), dma=False):
        o = Op(eng, fn, dma)
        deps = {}
        for b in reads:
            if b.w is not None:
                deps[id(b.w)] = b.w
            if b.excl:
                for k_, d in b.r.items():
                    if k_ != eng:
                        deps[id(d)] = d
        for b in writes:
            if b.w is not None:
                deps[id(b.w)] = b.w
            for d in b.r.values():
                deps[id(d)] = d
        if dma:
            lst = self.dma_ops[eng]
            k = len(lst)
            PS = self.dma_pool_size[eng]
            if k >= PS:
                d = lst[k - PS]
                deps[id(d)] = d
            lst.append(o)
        o.deps = list(deps.values())
        o.pos = len(self.ops[eng])
        self.ops[eng].append(o)
        for b in reads:
            key = ("dma", id(o)) if dma else eng
            b.r[key] = o
        for b in writes:
            b.w = o
            b.r = {}
        return o

    def _need_wait(self, o, d):
        if d.dma:
            return True
        if d.eng != o.eng:
            return True
        if o.dma:
            return True
        if o.eng == "pe":
            return False
        return (o.pos - d.pos) <= self.SAME_ENG_WINDOW

    def prepare(self):
        nc = self.nc
        for e in self.ENGS:
            for o in self.ops[e]:
                for d in o.deps:
                    if self._need_wait(o, d) and not d.dma:
                        d.signal = True
        self.esem = {e: nc.alloc_semaphore(f"s_{e}") for e in self.ENGS}
        for e in self.ENGS:
            c = 0
            for o in self.ops[e]:
                if not o.dma and o.signal:
                    c += 1
                    o.semval = c
        for e in self.ENGS:
            n = len(self.dma_ops[e])
            if n == 0:
                continue
            PS = self.dma_pool_size[e]
            sems = [nc.alloc_semaphore(f"d_{e}{i}") for i in range(min(PS, n))]
            for k, o in enumerate(self.dma_ops[e]):
                o.dsem = sems[k % PS]
                o.dval = 16 * (k // PS + 1)

    def emit_engine(self, e, eng):
        seen = {}
        for o in self.ops[e]:
            waits = {}
            for d in o.deps:
                if not self._need_wait(o, d):
                    continue
                if d.dma:
                    key = ("d", id(d.dsem))
                    sem, val = d.dsem, d.dval
                else:
                    key = ("e", d.eng)
                    sem, val = self.esem[d.eng], d.semval
                if seen.get(key, 0) >= val:
                    continue
                if key not in waits or waits[key][1] < val:
                    waits[key] = (sem, val)
            for key, (sem, val) in waits.items():
                eng.wait_ge(sem, val)
                seen[key] = val
            inst = o.fn(eng)
            if o.dma:
                inst.then_inc(o.dsem, 16)
            elif o.signal:
                inst.then_inc(self.esem[e], 1)


class Ring:
    def __init__(self, items):
        self.items = items
        self.i = 0

    def next(self):
        it = self.items[self.i % len(self.items)]
        self.i += 1
        return it


def _constants():
    ident = np.eye(128, dtype=np.float32)
    bd = np.zeros((128, 128), np.float32)
    bd[:64, :64] = 1.0
    bd[64:, 64:] = 1.0
    rm = np.zeros((128, 128), np.float32)
    for blk in (0, 64):
        for i in range(64):
            sec = i // 16
            if sec in (0, 2):
                rm[blk + i + 16, blk + i] = -1.0
            else:
                rm[blk + i - 16, blk + i] = 1.0
    t = np.arange(NS)
    row = (t // 64).astype(np.float64)
    col = (t % 64).astype(np.float64)
    inv = 10000.0 ** (-np.arange(16, dtype=np.float64) / 16)
    ang = np.zeros((128, NS), np.float64)
    for p in range(128):
        d = p % 64
        f = d % 16
        ang[p] = (row if d < 32 else col) * inv[f]
    cos = np.cos(ang).astype(np.float32)
    sin = np.sin(ang).astype(np.float32)
    neg = np.zeros((128, 384), np.float32)
    jl = np.arange(128)[:, None]
    il = np.arange(128)[None, :]
    neg[:, :] = 1.0
    neg[:, 0:128] = np.where(jl <= il, 1.0, 0.0)
    neg[:, 256:384] = np.where(il <= jl, 1.0, 0.0)
    bf = ml_dtypes.bfloat16
    return {
        "c_identf": ident,
        "c_identb": ident.astype(bf),
        "c_bd": bd.astype(bf),
        "c_rm": rm.astype(bf),
        "c_cos": cos,
        "c_sin": sin,
        "c_neg": neg,
    }


def _pair_cols(base, j):
    return list(range(base + j * 64, base + (j + 1) * 64)) + list(range(base + (4 + j) * 64, base + (5 + j) * 64))


def _win_perm():
    cols = []
    for j in range(4):
        cols += _pair_cols(0, j)
    for j in range(4):
        cols += _pair_cols(768, j)
    for j in range(4):
        cols += _pair_cols(1280, j)
    for j in range(4):
        cols += _pair_cols(2048, j)
    cols += list(range(512, 640)) + list(range(1792, 1920)) + list(range(640, 768)) + list(range(1920, 2048))
    return np.array(cols)


def _wout_perm():
    rows = []
    for j in range(4):
        rows += _pair_cols(0, j)
    for j in range(4):
        rows += _pair_cols(512, j)
    return np.array(rows)


CH_Q = {"A": 0, "B": 8}
CH_G = {"A": 4, "B": 12}
CH_K = {"A": 16, "B": 17}
CH_V = {"A": 18, "B": 19}


def build_nc():
    nc = bass.Bass("TRN2", target_bir_lowering=False)
    try:
        nc.allow_low_precision("bf16 matmul operands, fp32 accumulation (reference tolerance is bf16-level)")
    except Exception:
        pass
    P = Prog(nc)

    def din(name, shape, dt=F32):
        return nc.dram_tensor(name, list(shape), dt, kind="ExternalInput").ap()

    def dout(name, shape):
        return nc.dram_tensor(name, list(shape), F32, kind="ExternalOutput").ap()

    xs_d = din("xs", [NS, D])
    xp_d = din("xp", [NP, D])
    cka_d = din("cka", [256, 128])
    cva_d = din("cva", [256, 128])
    ckb_d = din("ckb", [256, 128])
    cvb_d = din("cvb", [256, 128])
    cond_d = din("cond", [128, 8, 2])
    wmod_d = din("wmod", [24, 128, 8 * 128])
    bmod_d = din("bmod", [128, 24])
    gain_d = din("gain", [128, 8])
    win_d = din("win", [20, 128, 8 * 128])
    wout_d = din("wout", [8, 128, 1024])
    gn_d = din("gn", [128, 4])
    sink_d = din("sink", [1, 4 * 128])
    knbc_d = din("knbc", [128, 2, 64])
    identf_d = din("c_identf", [128, 128])
    identb_d = din("c_identb", [128, 128], BF16)
    bd_d = din("c_bd", [128, 128], BF16)
    rm_d = din("c_rm", [128, 128], BF16)
    cos_d = din("c_cos", [128, NS])
    sin_d = din("c_sin", [128, NS])
    neg_d = din("c_neg", [128, 384])

    ys_d = dout("ys", [NS, D])
    yp_d = dout("yp", [NP, D])
    nk_d = {"A": dout("nka", [NP, 128]), "B": dout("nkb", [NP, 128])}
    nv_d = {"A": dout("nva", [NP, 128]), "B": dout("nvb", [NP, 128])}

    def sb(name, shape, dt=F32):
        return nc.alloc_sbuf_tensor(name, list(shape), dt).ap()

    hT = sb("hT", [128, 8, NS], BF16)
    omT = sb("omT", [128, 8, NS], BF16)
    kT = {"A": sb("kTa", [128, 256 + NS], BF16), "B": sb("kTb", [128, 256 + NS], BF16)}
    Vsb = sb("Vsb", [128, 18, 256], BF16)
    qT = [sb(f"qT{i}", [128, NS], BF16) for i in range(2)]
    sg = [sb(f"sg{i}", [128, NS], BF16) for i in range(2)]
    cos_t = [sb(f"cos{i}", [128, 512], F32) for i in range(2)]
    sin_t = [sb(f"sin{i}", [128, 512], F32) for i in range(2)]
    PT = [sb(f"PT{i}", [128, 2, 512], BF16) for i in range(3)]
    xr = [sb(f"xr{i}", [128, 1024], F32) for i in range(2)]
    xsb = [sb(f"xsb{i}", [128, 1024], BF16) for i in range(2)]
    wst = [sb(f"wst{i}", [128, 1024], F32) for i in range(3)]
    wbf = [sb(f"wbf{i}", [128, 8, 128], BF16) for i in range(4)]
    wv = sb("wv", [128, 8, 512], BF16)
    gate_bc = [sb(f"gatebc{i}", [128, 1024], F32) for i in range(2)]
    sq_t = [sb(f"sq{i}", [128, 512], BF16) for i in range(2)]
    rstd_t = [sb(f"rstd{i}", [128, 512], F32) for i in range(2)]
    kh_t = [sb(f"kh{i}", [128, 512], BF16) for i in range(2)]
    t1_t = [sb(f"t1_{i}", [128, 512], F32) for i in range(2)]
    t2_t = [sb(f"t2_{i}", [128, 512], F32) for i in range(2)]
    xc_t = [sb(f"xc{i}", [128, 512], F32) for i in range(2)]
    rY_t = [sb(f"rY{i}", [128, 512], F32) for i in range(2)]
    yb_t = [sb(f"yb{i}", [128, 512], F32) for i in range(2)]
    kout_t = [sb(f"kout{i}", [128, 256], F32) for i in range(1)]
    knbc = sb("knbc_sb", [128, 2, 64], F32)
    kst = sb("kst", [128, 8, 12], F32)
    kjunk = sb("kjunk", [128, 64], BF16)
    vout_t = [sb(f"vout{i}", [128, 256], F32) for i in range(2)]
    cst_t = sb("cst", [128, 4, 2, 128], F32)
    cbf_t = sb("cbf", [128, 2, 2, 128], BF16)
    identf = sb("identf", [128, 128], F32)
    identb = sb("identb", [128, 128], BF16)
    bd_t = sb("bd", [128, 128], BF16)
    rm_t = sb("rm", [128, 128], BF16)
    neg_t = sb("neg", [128, 384], F32)
    ones_b = sb("onesb", [128, 512], BF16)
    ones_f = sb("onesf", [128, 128], F32)
    diag_t = [sb(f"diag{i}", [128, 128], F32) for i in range(2)]
    s_sb = sb("s_sb", [128, 8, 2], F32)
    se_sb = sb("se_sb", [128, 8, 2], F32)
    mod_sb = sb("mod_sb", [128, 24, 2], F32)
    gp_sb = sb("gp_sb", [128, 8, 2], F32)
    bmod_sb = sb("bmod_sb", [128, 24], F32)
    gain_sb = sb("gain_sb", [128, 8], F32)
    gn_sb = sb("gn_sb", [128, 4], F32)
    sink_sb = sb("sink_sb", [1, 512], F32)
    esink = sb("esink", [1, 512], BF16)
    eps_t = sb("eps_t", [128, 1], F32)
    stats = sb("stats", [128, 24, 4], F32)

    S2 = [nc.alloc_psum_tensor(f"S{i}", [128, 2, 512], F32).ap() for i in range(2)]
    S2b = [a.bitcast(BF16) for a in S2]
    bank = [S2[0][:, 0, :], S2[0][:, 1, :], S2[1][:, 0, :], S2[1][:, 1, :]]
    bank += [nc.alloc_psum_tensor(f"bank{i}", [128, 512], F32).ap() for i in range(4, 8)]
    bankb = [P.buf(f"bank{i}", excl=True) for i in range(8)]

    def bank_bf(i, shape):
        a = S2b[i // 2][:, i % 2, :] if i < 4 else bank[i].bitcast(BF16)
        if len(shape) == 3:
            return a[:, 0:shape[1] * shape[2]].rearrange("p (a b) -> p a b", a=shape[1])
        return a[:, 0:shape[1]]

    b_hT = P.bufs(16, "hT")
    b_omT = [[P.buf(f"om{j}_{t}") for t in range(4)] for j in range(8)]
    b_kT = {m: P.bufs(5, f"kT{m}") for m in "AB"}
    b_V = P.bufs(18, "V")
    b_q = [P.bufs(4, f"q{i}_") for i in range(2)]
    b_sg = [P.bufs(4, f"sg{i}_") for i in range(2)]
    b_const = P.buf("const")
    b_mod = P.buf("mod")
    b_gate = P.bufs(2, "gate")
    b_stats = P.bufs(24, "stats")
    b_wv = P.buf("wv")

    def rng(items):
        return Ring(items)

    b_cst = P.bufs(4, "cst")
    _cst2 = cst_t.rearrange("p a b c -> p (a b c)")
    pc_t = [_cst2[:, 0:512], _cst2[:, 512:1024]]
    pc_b = [[b_cst[0], b_cst[1]], [b_cst[2], b_cst[3]]]

    PT_r = rng(list(zip(PT, P.bufs(3, "PT"))))
    xr_r = rng(list(zip(xr, P.bufs(2, "xr"))))
    xsb_r = rng(list(zip(xsb, P.bufs(2, "xsb"))))
    wst_r = rng(list(zip(wst, P.bufs(3, "wst"))))
    wbf_r = rng(list(zip(wbf, P.bufs(4, "wbf"))))
    sq_r = rng(list(zip(sq_t, P.bufs(2, "sq"))))
    rstd_r = rng(list(zip(rstd_t, P.bufs(2, "rstd"))))
    kh_r = rng(list(zip(kh_t, P.bufs(2, "kh"))))
    t1_r = rng(list(zip(t1_t, P.bufs(2, "t1"))))
    t2_r = rng(list(zip(t2_t, P.bufs(2, "t2"))))
    xc_r = rng(list(zip(xc_t, P.bufs(2, "xc"))))
    rY_r = rng(list(zip(rY_t, P.bufs(2, "rY"))))
    _ybl = list(zip(yb_t, P.bufs(2, "yb")))
    yb_r = rng(_ybl)
    eg_r = rng(_ybl)
    kout_r = rng(list(zip(kout_t, P.bufs(1, "kout"))))
    vout_r = rng(list(zip(vout_t, P.bufs(2, "vout"))))
    diag_r = rng(list(zip(diag_t, P.bufs(2, "diag"))))
    b_cs = P.bufs(2, "cs")
    b_sn = P.bufs(2, "sn")
    out_dma_bufs = []

    def dma(out, in_, reads=(), writes=(), is_out=False, q="sp"):
        if is_out:
            b = P.buf()
            out_dma_bufs.append(b)
            writes = list(writes) + [b]
            q = "pool"
        return P.op(q, lambda e: e.dma_start(out=out, in_=in_), reads, writes, dma=True)

    def mm(out, lhsT, rhs, start, stop, reads, writes, tp=None):
        if tp is None:
            return P.op("pe", lambda e: e.matmul(out, lhsT=lhsT, rhs=rhs, start=start, stop=stop), reads, writes)
        return P.op("pe", lambda e: e.matmul(out, lhsT=lhsT, rhs=rhs, start=start, stop=stop, tile_position=tp), reads, writes)

    def tr(out, in_, ident, reads, writes):
        return P.op("pe", lambda e: e.transpose(out, in_, ident), reads, writes)

    def act(out, in_, func, reads, writes, bias=None, scale=1.0, accum_out=None):
        kw = {}
        if bias is not None:
            kw["bias"] = bias
        if accum_out is not None:
            kw["accum_out"] = accum_out
        return P.op("act", lambda e: e.activation(out=out, in_=in_, func=func, scale=scale, **kw), reads, writes)

    def ts(eng, out, in0, s1, s2, op0, op1, reads, writes):
        if s2 is None:
            return P.op(eng, lambda e: e.tensor_scalar(out=out, in0=in0, scalar1=s1, scalar2=None, op0=op0), reads, writes)
        return P.op(eng, lambda e: e.tensor_scalar(out=out, in0=in0, scalar1=s1, scalar2=s2, op0=op0, op1=op1), reads, writes)

    def tt(eng, out, in0, in1, op, reads, writes):
        return P.op(eng, lambda e: e.tensor_tensor(out=out, in0=in0, in1=in1, op=op), reads, writes)

    def stt(eng, out, in0, scalar, in1, op0, op1, reads, writes):
        return P.op(eng, lambda e: e.scalar_tensor_tensor(out=out, in0=in0, scalar=scalar, in1=in1, op0=op0, op1=op1), reads, writes)

    def cp(eng, out, in_, reads, writes):
        return P.op(eng, lambda e: e.tensor_copy(out=out, in_=in_), reads, writes)

    def recip(out, in_, reads, writes):
        return P.op("dve", lambda e: e.reciprocal(out=out, in_=in_), reads, writes)

    def memset(eng, ap, val, writes):
        return P.op(eng, lambda e: e.memset(ap, val), (), writes)

    b_c = {k: P.buf(k) for k in ("identf", "identb", "bd", "rm", "cos", "sin", "neg", "bmod", "gain", "gn", "sink", "s")}
    dma(identf, identf_d, writes=[b_c["identf"]])
    dma(identb, identb_d, writes=[b_c["identb"]])
    dma(s_sb, cond_d, writes=[b_c["s"]])
    dma(bmod_sb, bmod_d, writes=[b_c["bmod"]])
    dma(gain_sb, gain_d, writes=[b_c["gain"]])
    dma(gn_sb, gn_d, writes=[b_c["gn"]])
    dma(sink_sb, sink_d, writes=[b_c["sink"]])
    b_knbc = P.buf("knbc")
    dma(knbc, knbc_d, writes=[b_knbc])
    b_kst = P.buf("kst")
    memset("dve", kst, 0.0, [b_kst])
    b_ones = P.buf("ones")
    b_eps = P.buf("eps")
    memset("pool", ones_b, 1.0, [b_ones])
    memset("pool", ones_f, 1.0, [b_ones])
    memset("dve", eps_t, EPS, [b_eps])
    for i in range(24):
        pass
    memset("dve", stats, 0.0, b_stats)

    act(se_sb, s_sb, AF.Exp, [b_c["s"]], [b_mod], scale=-1.0)
    ts("dve", se_sb, se_sb, 1.0, None, ALU.add, None, [b_mod], [b_mod])
    recip(se_sb, se_sb, [b_mod], [b_mod])
    tt("dve", s_sb, s_sb, se_sb, ALU.mult, [b_mod, b_c["s"]], [b_c["s"]])

    mod_ps = bank[4][:, 0:48].rearrange("p (a b) -> p a b", a=24)
    s_bf = sb("s_bf", [128, 8, 2], BF16)
    cp("dve", s_bf, s_sb, [b_c["s"]], [b_c["s"]])
    mod_ring = rng(list(wst_r.items) + list(xr_r.items))
    for nch in range(24):
        wa, wb = mod_ring.next()
        dma(wa, wmod_d[nch], writes=[wb])
        ba_, bb_ = wbf_r.next()
        cp("dve", ba_.rearrange("p a b -> p (a b)"), wa, [wb], [bb_])
        for kc in range(8):
            mm(mod_ps[:, nch, :], ba_[:, kc, :], s_bf[:, kc, :], kc == 0, kc == 7, [bb_, b_c["s"]], [bankb[4]])
    dma(bd_t, bd_d, writes=[b_c["bd"]])
    dma(rm_t, rm_d, writes=[b_c["rm"]])
    dma(neg_t, neg_d, writes=[b_c["neg"]])
    for m in range(2):
        tt("dve", mod_sb[:, :, m], mod_ps[:, :, m], bmod_sb, ALU.add, [bankb[4], b_c["bmod"]], [b_mod])
    for m in range(2):
        stt("dve", gp_sb[:, :, m], mod_sb[:, 8:16, m], 1.0, gain_sb, ALU.add, ALU.mult, [b_mod, b_c["gain"]], [b_mod])
    for m in range(2):
        for c in range(8):
            da, db = diag_r.next()
            ts("dve", da, identf, mod_sb[:, 16 + c, m:m + 1], None, ALU.mult, None, [b_mod, b_c["identf"]], [db])
            bk = 5 + (c // 4)
            mm(bank[bk][:, (c % 4) * 128:(c % 4 + 1) * 128], ones_f, da, True, True, [b_ones, db], [bankb[bk]])
        for hlf in range(2):
            cp("dve", gate_bc[m][:, hlf * 512:(hlf + 1) * 512], bank[5 + hlf], [bankb[5 + hlf]], [b_gate[m]])
    act(esink, sink_sb, AF.Exp, [b_c["sink"]], [b_c["sink"]])

    stat_idx = [0]

    def preproc(xd, ntiles, m, tpb=(0, 1)):
        tp_banks = rng(list(tpb))

        def stage_a(t):
            xa, xb = xr_r.next()
            dma(xa, xd[t * 128:(t + 1) * 128, :], writes=[xb])
            xsa, xsbb = xsb_r.next()
            k = stat_idx[0]
            stat_idx[0] += 1
            bst = b_stats[k]
            act(xsa, xa, AF.Square, [xb], [xsbb, bst], accum_out=stats[:, k, 0:1])
            act(stats[:, k, 1:2], stats[:, k, 0:1], AF.Ln, [bst, b_eps], [bst], bias=eps_t, scale=1.0 / D)
            act(stats[:, k, 2:3], stats[:, k, 1:2], AF.Exp, [bst], [bst], scale=-0.5)
            act(xsa, xa, AF.Copy, [xb, bst], [xsbb], scale=stats[:, k, 2:3])
            return xsa, xsbb

        def stage_b(t, xsa, xsbb):
            bi = tp_banks.next()
            tpa = bank_bf(bi, [128, 8, 128])
            for c in range(8):
                tr(tpa[:, c, :], xsa[:, c * 128:(c + 1) * 128], identb, [xsbb, b_c["identb"]], [bankb[bi]])
            for c in range(8):
                ts("dve", hT[:, c, t * 128:(t + 1) * 128], tpa[:, c, :], gp_sb[:, c, m:m + 1], mod_sb[:, c, m:m + 1],
                   ALU.mult, ALU.add, [bankb[bi], b_mod], [b_hT[t]])

        nxt = stage_a(0)
        for t in range(ntiles):
            cur = nxt
            if t + 1 < ntiles:
                nxt = stage_a(t + 1)
            stage_b(t, *cur)
            yield

    def load_w(ch):
        wa, wb = wst_r.next()
        dma(wa, win_d[ch], writes=[wb])
        ba, bb = wbf_r.next()
        cp("dve", ba.rearrange("p a b -> p (a b)"), wa, [wb], [bb])
        return ba, bb

    def load_wv(with_k):
        if _ENV.get("KDBG", "") == "h":
            with_k = False
        chs = [CH_V["A"], CH_V["B"]] + ([CH_K["A"], CH_K["B"]] if with_k else [])
        for i, ch in enumerate(chs):
            wa, wb = wst_r.next()
            dma(wa, win_d[ch], writes=[wb])
            cp("dve", wv[:, :, i * 128:(i + 1) * 128], wa.rearrange("p (a b) -> p a b", a=8), [wb], [b_wv])

    PB, SRB = 6, 7

    def proj_tile_g(w, tc, kind, gncol=None, rope=False, dst=None, dst_b=None, tok0=0, PB=6, SRB=7, slot=0, act_recip=False):
        wa, wb = w
        pa = bank[PB]
        toks = slice(tc * 512, (tc + 1) * 512)
        hb = b_hT[tc * 4:(tc + 1) * 4]
        if rope:
            dma(cos_t[slot], cos_d[:, tok0:tok0 + 512], writes=[b_cs[slot]], q="pool")
            dma(sin_t[slot], sin_d[:, tok0:tok0 + 512], writes=[b_sn[slot]], q="pool")
        for kc in range(8):
            mm(pa, wa[:, kc, :], hT[:, kc, toks], kc == 0, kc == 7, [wb] + hb, [bankb[PB]])
            if kc % 2 == 1:
                yield
        pc = pc_t[slot]
        pcb = pc_b[slot]
        cp("dve", pc, pa, [bankb[PB]], pcb)
        if kind == "g":
            yield
            ea, eb = _ybl[slot]
            act(ea, pc, AF.Exp, pcb, [eb], scale=-1.0)
            yield
            if act_recip:
                act(ea, ea, AF.Ln, [eb, b_ones], [eb], bias=ones_f[:, 0:1], scale=1.0)
                act(ea, ea, AF.Exp, [eb], [eb], scale=-1.0)
                yield
            else:
                ts("dve", ea, ea, 1.0, None, ALU.add, None, [eb], [eb])
                recip(ea, ea, [eb], [eb])
            tt("dve", dst, pc, ea, ALU.mult, pcb + [eb], dst_b)
            yield
            return
        sa, sbb = sq_r.items[slot]
        tt("dve", sa, pc, pc, ALU.mult, pcb, [sbb])
        yield
        mm(bank[SRB], bd_t, sa, True, True, [b_c["bd"], sbb], [bankb[SRB]])
        yield
        ra, rb = rstd_r.items[slot]
        act(ra, bank[SRB], AF.Ln, [bankb[SRB], b_eps], [rb], bias=eps_t, scale=1.0 / 64)
        act(ra, ra, AF.Exp, [rb], [rb], scale=-0.5)
        yield
        gcol = gn_sb[:, gncol:gncol + 1]
        if rope:
            ka, kb = kh_r.items[slot]
            stt("dve", ka, pc, gcol, ra, ALU.mult, ALU.mult, pcb + [rb, b_c["gn"]], [kb])
            yield
            mm(bank[SRB], rm_t, ka, True, True, [b_c["rm"], kb], [bankb[SRB]])
            yield
            t1a, t1b = t1_r.items[slot]
            t2a, t2b = t2_r.items[slot]
            tsl = slice(tok0, tok0 + 512)
            tt("dve", t1a, ka, cos_t[slot], ALU.mult, [kb, b_cs[slot]], [t1b])
            tt("dve", t2a, bank[SRB], sin_t[slot], ALU.mult, [bankb[SRB], b_sn[slot]], [t2b])
            tt("dve", dst, t1a, t2a, ALU.add, [t1b, t2b], dst_b)
            yield
        else:
            stt("dve", dst, pc, gcol, ra, ALU.mult, ALU.mult, pcb + [rb, b_c["gn"]], dst_b)
            yield

    def v_tiles(ntiles, vt_of, prompt):
        vb_r = rng([2, 3])
        ncol = 512 if (prompt and _ENV.get("KDBG", "") != "h") else 256
        for t in range(ntiles):
            bi = vb_r.next()
            va = bank[bi][:, 0:ncol]
            for kc in range(8):
                mm(va, hT[:, kc, t * 128:(t + 1) * 128], wv[:, kc, 0:ncol], kc == 0, kc == 7, [b_hT[t], b_wv], [bankb[bi]])
            vt = vt_of(t)
            P.op("act", lambda e, o=Vsb[:, vt, :], i=va[:, 0:256]: e.copy(out=o, in_=i), [bankb[bi]], [b_V[vt]])
            if prompt:
                voa, vob = vout_r.next()
                if not _ENV.get("KNOVCP"):
                    cp("dve", voa, va[:, 0:256], [bankb[bi]] + ([b_V[vt]] if _ENV.get("KSERIAL") else []), [vob])
                if not _ENV.get("KNOVDMA"):
                    dma(nv_d["A"][t * 128:(t + 1) * 128, :], voa[:, 0:128], reads=[vob], is_out=True)
                    dma(nv_d["B"][t * 128:(t + 1) * 128, :], voa[:, 128:256], reads=[vob], is_out=True)
                if _ENV.get("KDBG", "") in ("g", "h"):
                    yield
                    continue
                for hh in range(4):
                    act(kjunk, va[:, 256 + hh * 64:256 + (hh + 1) * 64], AF.Square, [bankb[bi]], [b_kst], accum_out=kst[:, t, hh:hh + 1])
                act(kst[:, t, 4:8], kst[:, t, 0:4], AF.Ln, [b_kst, b_eps], [b_kst], bias=eps_t, scale=1.0 / 64)
                act(kst[:, t, 8:12], kst[:, t, 4:8], AF.Exp, [b_kst], [b_kst], scale=-0.5)
                koa, kob = kout_r.next()
                for hh in range(4):
                    stt("dve", koa[:, hh * 64:(hh + 1) * 64], va[:, 256 + hh * 64:256 + (hh + 1) * 64], kst[:, t, 8 + hh:9 + hh],
                        knbc[:, hh // 2, :], ALU.mult, ALU.mult, [bankb[bi], b_kst, b_knbc], [kob])
                dma(nk_d["A"][t * 128:(t + 1) * 128, :], koa[:, 0:128], reads=[kob], is_out=True)
                dma(nk_d["B"][t * 128:(t + 1) * 128, :], koa[:, 128:256], reads=[kob], is_out=True)
            yield

    SB_ = [(0, 1), (2, 3)]
    XB, YB = 4, 5

    class Chunk:
        pass

    def make_chunk(mx, j, qa, qcol0, n, tiles, sga, sg_b, om_dst, om_b, q_b, sink=None):
        c = Chunk()
        c.sink = sink
        c.mx, c.j, c.qa, c.qcol0, c.n, c.tiles = mx, j, qa, qcol0, n, tiles
        c.sga, c.sg_b, c.om_dst, c.om_b, c.q_b = sga, sg_b, om_dst, om_b, q_b
        return c

    def att_stream(chunks, sink, s_slots=(0, 1), act_recip=False):
        s_ring = rng(list(s_slots))
        flat = [(c, i) for c in chunks for i in range(len(c.tiles))]
        sbuf_of = {}
        pt_of = {}

        def qk(t):
            c, i = flat[t]
            kT_ap, k_b, V_ap, v_b, q0, n, negsl = c.tiles[i]
            si = s_ring.next()
            sbuf_of[t] = si
            b0, b1 = SB_[si]
            qcols = slice(c.qcol0 + q0, c.qcol0 + q0 + n)
            for h, bk in ((0, b0), (1, b1)):
                rows = slice(h * 64, (h + 1) * 64)
                mm(bank[bk][:, 0:n], kT_ap[rows, :], c.qa[rows, qcols], True, True, list(k_b) + c.q_b, [bankb[bk]])

        def ex(t):
            c, i = flat[t]
            n = c.tiles[i][5]
            b0, b1 = SB_[sbuf_of[t]]
            pa, pb = PT_r.next()
            pt_of[t] = (pa, pb)
            src = S2[sbuf_of[t]][:, :, 0:n]
            act(pa[:, :, 0:n], src, AF.Exp, [bankb[b0], bankb[b1]], [pb], scale=0.125)
            negsl = c.tiles[i][6]
            if negsl is not None:
                for h in range(2):
                    tt("dve", pa[:, h, 0:n], pa[:, h, 0:n], neg_t[:, negsl], ALU.mult, [pb, b_c["neg"]], [pb])

        def pv(t):
            c, i = flat[t]
            kT_ap, k_b, V_ap, v_b, q0, n, negsl = c.tiles[i]
            pa, pb = pt_of[t]
            first = i == 0
            last = i == len(c.tiles) - 1
            cols = slice(q0, q0 + n)
            csink = sink if c.sink is None else c.sink
            if first and csink:
                mm(bank[YB][:, 0:c.n], esink[0:1, c.j * 128:(c.j + 1) * 128], ones_b[0:1, 0:c.n], True, False,
                   [b_c["sink"], b_ones], [bankb[YB]])
            for h in range(2):
                rows = slice(h * 64, (h + 1) * 64)
                mm(bank[YB][rows, cols], ones_b[:, rows], pa[:, h, 0:n], first and not csink, last, [b_ones, pb], [bankb[YB]],
                   tp=(0, h * 64))
            for h in range(2):
                rows = slice(h * 64, (h + 1) * 64)
                mm(bank[XB][rows, cols], V_ap[:, rows], pa[:, h, 0:n], first, last, list(v_b) + [pb], [bankb[XB]], tp=(0, h * 64))
            if last:
                n_ = c.n
                ra, rb = rY_r.next()
                xa_, xb_ = xc_r.next()
                if act_recip:
                    act(ra[:, 0:n_], bank[YB][:, 0:n_], AF.Ln, [bankb[YB]], [rb])
                    cp("dve", xa_[:, 0:n_], bank[XB][:, 0:n_], [bankb[XB]], [xb_])
                    act(ra[:, 0:n_], ra[:, 0:n_], AF.Exp, [rb], [rb], scale=-1.0)
                else:
                    cp("dve", ra[:, 0:n_], bank[YB][:, 0:n_], [bankb[YB]], [rb])
                    cp("dve", xa_[:, 0:n_], bank[XB][:, 0:n_], [bankb[XB]], [xb_])
                    recip(ra[:, 0:n_], ra[:, 0:n_], [rb], [rb])
                tt("dve", ra[:, 0:n_], ra[:, 0:n_], c.sga, ALU.mult, [rb] + c.sg_b, [rb])
                tt("dve", c.om_dst, xa_[:, 0:n_], ra[:, 0:n_], ALU.mult, [xb_, rb], c.om_b)

        T = len(flat)
        L = len(s_slots)
        for t in range(min(L, T)):
            qk(t)
        for t in range(T):
            ex(t)
            if L == 1:
                if t + 1 < T:
                    qk(t + 1)
                pv(t)
            else:
                pv(t)
                if t + L < T:
                    qk(t + L)
            yield

    def outproj(ntiles, xd, yd, m, wout_b, obanks=(0, 1, 2, 3)):
        ob_r = rng(list(obanks))
        for t in range(ntiles):
            xa, xb = xr_r.next()
            dma(xa, xd[t * 128:(t + 1) * 128, :], writes=[xb])
            for hlf in range(2):
                bi = ob_r.next()
                cols = slice(hlf * 512, (hlf + 1) * 512)
                for j in range(8):
                    mm(bank[bi], omT[:, j, t * 128:(t + 1) * 128], wout_ap(j)[:, cols], j == 0, j == 7,
                       [b_omT[j][t // 4]] + list(wout_b[j]), [bankb[bi]])
                ya, yb = yb_r.next()
                tt("dve", ya, bank[bi], gate_bc[m][:, cols], ALU.mult, [bankb[bi], b_gate[m]], [yb])
                tt("dve", ya, ya, xa[:, cols], ALU.add, [xb, yb], [yb])
                dma(yd[t * 128:(t + 1) * 128, cols], ya, reads=[yb], is_out=True)
            yield

    def wout_ap(j):
        return hT[:, j, 1024:2048]

    def load_wout():
        bl = []
        for j in range(8):
            wa, wb = wst_r.next()
            dma(wa, wout_d[j], writes=[wb])
            tiles = b_hT[(j % 2) * 8:(j % 2 + 1) * 8]
            cp("dve", wout_ap(j), wa, [wb], tiles)
            bl.append(tiles)
        return bl

    def run(gen):
        for _ in gen:
            pass

    b_cbf = P.bufs(2, "cbf")

    def ctx_prep():
        for idx, d in enumerate((cka_d, cva_d, ckb_d, cvb_d)):
            dma(cst_t[:, idx, :, :], d.rearrange("(i p) f -> p i f", p=128), writes=[b_cst[idx]])
        for mi, mx in enumerate("AB"):
            cp("dve", cbf_t[:, mi, :, :], cst_t[:, 2 * mi, :, :], [b_cst[2 * mi]], [b_cbf[mi]])
            for i in range(2):
                pt = bank_bf(i, [128, 128])
                tr(pt, cbf_t[:, mi, i, :], identb, [b_cbf[mi], b_c["identb"]], [bankb[i]])
                cp("dve", kT[mx][:, i * 128:(i + 1) * 128], pt, [bankb[i]], [b_kT[mx][0]])
            for i in range(2):
                cp("dve", Vsb[:, i, mi * 128:(mi + 1) * 128], cst_t[:, 2 * mi + 1, i, :], [b_cst[2 * mi + 1]], [b_V[i]])

    def sample_chunks(mx, j, buf):
        mcol = 0 if mx == "A" else 128
        jidx = (0 if mx == "A" else 4) + j
        chunks = []
        for c in range(4):
            tiles = []
            for i in range(2):
                tiles.append((kT[mx][:, i * 128:(i + 1) * 128], [b_kT[mx][0]], Vsb[:, i, mcol:mcol + 128], [b_V[i]], 0, 512, None))
            if mx == "A":
                kts = range(16)
            else:
                kts = range(max(0, 4 * c - 1), min(15, 4 * c + 4) + 1)
            for kt in kts:
                kap = kT[mx][:, 256 + kt * 128:256 + (kt + 1) * 128]
                kb = [b_kT[mx][1 + kt // 4]]
                vap = Vsb[:, 2 + kt, mcol:mcol + 128]
                vb = [b_V[2 + kt]]
                if mx == "A":
                    tiles.append((kap, kb, vap, vb, 0, 512, None))
                else:
                    qbs = [qb for qb in (kt - 1, kt, kt + 1) if 4 * c <= qb <= 4 * c + 3]
                    q0 = (qbs[0] - 4 * c) * 128
                    n = 128 * len(qbs)
                    rels = [qb - kt for qb in qbs]
                    negsl = slice((rels[0] + 1) * 128, (rels[-1] + 2) * 128)
                    if all(r == 0 for r in rels):
                        negsl = None
                    tiles.append((kap, kb, vap, vb, q0, n, negsl))
            chunks.append(make_chunk(mx, j, qT[buf], c * 512, 512, tiles, sg[buf][:, c * 512:(c + 1) * 512], [b_sg[buf][c]],
                                     omT[:, jidx, c * 512:(c + 1) * 512], [b_omT[jidx][c]], [b_q[buf][c]]))
        return chunks

    def prompt_chunks(mx, j, buf, colbase=0, tokbase=0):
        mcol = 0 if mx == "A" else 128
        jidx = (0 if mx == "A" else 4) + j
        chunks = []
        for pb in range(4):
            tiles = []
            for i in range(2):
                t = pb * 2 + i
                tiles.append((kT[mx][:, t * 128:(t + 1) * 128], b_kT[mx], Vsb[:, t, mcol:mcol + 128], [b_V[t]], 0, 256, None))
            c0 = colbase + pb * 256
            chunks.append(make_chunk(mx, j, qT[buf], c0, 256, tiles, sg[buf][:, c0:c0 + 256], [b_sg[buf][tokbase + pb // 2]],
                                     omT[:, jidx, pb * 256:(pb + 1) * 256], [b_omT[jidx][pb // 2]], [b_q[buf][tokbase + pb // 2]],
                                     sink=(mx == "B")))
        return chunks

    pair_ctr = [0]

    def mix(fg, bg, ratio):
        acc = 0.0
        alive = bg is not None
        for _ in fg:
            acc += 1.0 / ratio
            while alive and acc >= 1.0:
                acc -= 1.0
                try:
                    next(bg)
                except StopIteration:
                    alive = False
        if alive:
            for _ in bg:
                pass

    def chains_g(specs, bank_pairs):
        pending = list(specs)
        active = [None] * len(bank_pairs)
        while pending or any(a is not None for a in active):
            for i, bp in enumerate(bank_pairs):
                if active[i] is None and pending:
                    active[i] = pending.pop(0)(bp[0], bp[1], i)
                if active[i] is not None:
                    try:
                        next(active[i])
                    except StopIteration:
                        active[i] = None
                        continue
                    yield

    def pair_specs(sample, mx, j, buf, ntc, wq, wg, act_recip):
        specs = []
        for tc in range(ntc):
            specs.append(lambda pb, srb, slot, tc=tc: proj_tile_g(wq, tc, "q", gncol=(0 if mx == "A" else 2), rope=sample,
                                                            dst=qT[buf][:, tc * 512:(tc + 1) * 512], dst_b=[b_q[buf][tc]],
                                                            tok0=tc * 512, PB=pb, SRB=srb, slot=slot))
            specs.append(lambda pb, srb, slot, tc=tc: proj_tile_g(wg, tc, "g", dst=sg[buf][:, tc * 512:(tc + 1) * 512],
                                                            dst_b=[b_sg[buf][tc]], PB=pb, SRB=srb, slot=slot, act_recip=act_recip))
        return specs

    def pair_proj_g(sample, mx, j, buf, ntc, bank_pairs, act_recip):
        wq = load_w(CH_Q[mx] + j)
        wg = load_w(CH_G[mx] + j)
        yield
        yield from chains_g(pair_specs(sample, mx, j, buf, ntc, wq, wg, act_recip), bank_pairs)

    def k_specs(sample, wk, ntc):
        specs = []
        for tc in range(ntc):
            for mx in "AB":
                gcol = 1 if mx == "A" else 3
                if sample:
                    specs.append(lambda pb, srb, slot, tc=tc, mx=mx, gcol=gcol: proj_tile_g(
                        wk[mx], tc, "k", gncol=gcol, rope=True, dst=kT[mx][:, 256 + tc * 512:256 + (tc + 1) * 512],
                        dst_b=[b_kT[mx][1 + tc]], tok0=tc * 512, PB=pb, SRB=srb, slot=slot))
                else:
                    specs.append(lambda pb, srb, slot, tc=tc, mx=mx, gcol=gcol: proj_tile_g(
                        wk[mx], tc, "k", gncol=gcol, rope=False, dst=kT[mx][:, tc * 512:(tc + 1) * 512],
                        dst_b=b_kT[mx], PB=pb, SRB=srb, slot=slot))
        return specs

    def wout_g(holder):
        bl = []
        for j in range(8):
            wa, wb = wst_r.next()
            dma(wa, wout_d[j], writes=[wb])
            tiles = b_hT[8:16]
            cp("dve", wout_ap(j), wa, [wb], tiles)
            bl.append(tiles)
            yield
        holder.extend(bl)

    def mix_g(fg, bg, ratio):
        acc = 0.0
        alive = bg is not None
        for _ in fg:
            acc += 1.0 / ratio
            while alive and acc >= 1.0:
                acc -= 1.0
                try:
                    next(bg)
                except StopIteration:
                    alive = False
            yield
        if alive:
            for _ in bg:
                yield

    def take(gens, n):
        for _ in range(n):
            for g in gens:
                try:
                    next(g)
                except StopIteration:
                    pass
            yield

    def front_g(sample, tpb, extra_specs=None):
        m = 0 if sample else 1
        xd, ntiles = (xs_d, 16) if sample else (xp_d, 8)
        ntc = ntiles // 4
        wk = {"A": load_w(CH_K["A"]), "B": load_w(CH_K["B"])}
        load_wv(not sample)
        pre = preproc(xd, ntiles, m, tpb)
        vgen = v_tiles(ntiles, (lambda t: 2 + t) if sample else (lambda t: t), not sample)
        ksp = k_specs(sample, wk, ntc)
        esp = extra_specs() if extra_specs is not None else []
        kbanks = [(6, 6), (7, 7)]

        def sp(tc):
            out = ksp[2 * tc:2 * tc + 2]
            if esp:
                out = out + esp[2 * tc:2 * tc + 2]
            return out

        nst = 2 * (10 if sample else 8) + (18 if esp else 0)
        yield from take([pre], 4)
        if sample:
            ctx_prep()
        for g in range(1, ntc):
            yield from mix_g(take([pre, vgen], 4), chains_g(sp(g - 1), kbanks), 4.0 / nst)
        yield from mix_g(take([vgen], 4), chains_g(sp(ntc - 1), kbanks), 4.0 / nst)

    def run_all():
        ntc = 4
        pairs = [(mx, j) for mx in "AB" for j in range(4)]
        bufs = []
        for _ in pairs:
            bufs.append(pair_ctr[0] % 2)
            pair_ctr[0] += 1

        def first_specs():
            wq = load_w(CH_Q["A"] + 0)
            wg = load_w(CH_G["A"] + 0)
            return pair_specs(True, "A", 0, bufs[0], ntc, wq, wg, True)

        run(front_g(True, (0, 1), first_specs))
        nbg = 1 + ntc * (10 + 8)
        wout_hold = []
        for idx, (mx, j) in enumerate(pairs):
            chunks = sample_chunks(mx, j, bufs[idx])
            nfg = sum(len(c.tiles) for c in chunks)
            wide = mx == "A"
            bps = [(6, 6), (7, 7)]
            if idx + 1 < len(pairs):
                bg = pair_proj_g(True, pairs[idx + 1][0], pairs[idx + 1][1], bufs[idx + 1], ntc, bps, not wide)
                ratio = max(nfg / nbg, 1.0 / (1.5 * len(bps)))
            else:
                bg = wout_g(wout_hold)
                ratio = nfg / 8
            mix(att_stream(chunks, sink=(mx == "B"), s_slots=(0, 1), act_recip=not wide), bg, ratio)
        kb2 = [(6, 6), (7, 7)]

        def unit_proj_g(j, buf):
            w = {}
            for mx in "AB":
                w[(mx, "q")] = load_w(CH_Q[mx] + j)
                w[(mx, "g")] = load_w(CH_G[mx] + j)
            yield
            specs = []
            for tc in range(2):
                for mi, mx in enumerate("AB"):
                    c0 = mi * 1024 + tc * 512
                    specs.append(lambda pb, srb, slot, tc=tc, mi=mi, mx=mx, c0=c0: proj_tile_g(
                        w[(mx, "q")], tc, "q", gncol=(0 if mx == "A" else 2), rope=False, dst=qT[buf][:, c0:c0 + 512],
                        dst_b=[b_q[buf][mi * 2 + tc]], PB=pb, SRB=srb, slot=slot))
                    specs.append(lambda pb, srb, slot, tc=tc, mi=mi, mx=mx, c0=c0: proj_tile_g(
                        w[(mx, "g")], tc, "g", dst=sg[buf][:, c0:c0 + 512], dst_b=[b_sg[buf][mi * 2 + tc]],
                        PB=pb, SRB=srb, slot=slot, act_recip=True))
            yield from chains_g(specs, kb2)

        ubuf = []
        for _ in range(4):
            ubuf.append(pair_ctr[0] % 2)
            pair_ctr[0] += 1

        def seq_g(*gens):
            for g in gens:
                yield from g

        mix(outproj(16, xs_d, ys_d, 0, wout_hold, obanks=(0, 1)),
            seq_g(front_g(False, (4, 5)), unit_proj_g(0, ubuf[0])), 16.0 / 126.0)

        for u in range(4):
            ca = prompt_chunks("A", u, ubuf[u], 0, 0)
            cb = prompt_chunks("B", u, ubuf[u], 1024, 2)
            chunks = [c for pr in zip(ca, cb) for c in pr]
            nfg = sum(len(c.tiles) for c in chunks)
            bg = unit_proj_g(u + 1, ubuf[u + 1]) if u + 1 < 4 else None
            mix(att_stream(chunks, sink=False, s_slots=(0, 1), act_recip=True), bg, max(nfg / 65.0, 1.0 / 3.0))
        run(outproj(8, xp_d, yp_d, 1, wout_hold))

    run_all()
    for _i in range(int(_ENV.get("KDUMMY", "0"))):
        mm(bank[6][:, 0:128], identb, identb, True, True, [b_c["identb"]], [bankb[6]])
    P.op("sp", lambda e: e.nop(), reads=out_dma_bufs)

    P.prepare()
    with nc.Block() as block:

        @block.tensor
        def _(e):
            P.emit_engine("pe", e)

        @block.scalar
        def _(e):
            P.emit_engine("act", e)

        @block.vector
        def _(e):
            P.emit_engine("dve", e)

        @block.gpsimd
        def _(e):
            P.emit_engine("pool", e)

        @block.sync
        def _(e):
            P.emit_engine("sp", e)

    return nc, P


_CACHE = {}


def kernel(x_prompt, x_sample, cache_k_a, cache_v_a, cache_k_b, cache_v_b, c, c_ctx,
           w_mod, b_mod, norm_gain, w_in, qn_a, kn_a, qn_b, kn_b, sink_b, w_out):
    f = lambda a: np.ascontiguousarray(np.asarray(a), dtype=np.float32)
    x_prompt, x_sample = f(x_prompt), f(x_sample)
    cache_k_a, cache_v_a, cache_k_b, cache_v_b = f(cache_k_a), f(cache_v_a), f(cache_k_b), f(cache_v_b)
    c, c_ctx, w_mod, b_mod, norm_gain, w_in = f(c), f(c_ctx), f(w_mod), f(b_mod), f(norm_gain), f(w_in)
    qn_a, kn_a, qn_b, kn_b, sink_b, w_out = f(qn_a), f(kn_a), f(qn_b), f(kn_b), f(sink_b), f(w_out)

    if "nc" not in _CACHE:
        _CACHE["nc"] = build_nc()[0]
        _CACHE["consts"] = _constants()
    nc = _CACHE["nc"]
    consts = _CACHE["consts"]

    wmod_l = np.ascontiguousarray(w_mod[0].reshape(8, 128, 24, 128).transpose(2, 1, 0, 3)).reshape(24, 128, 1024)
    win_p = w_in[0][:, _win_perm()]
    win_l = np.ascontiguousarray(win_p.reshape(8, 128, 20, 128).transpose(2, 1, 0, 3)).reshape(20, 128, 1024)
    wout_l = np.ascontiguousarray(w_out[0][_wout_perm(), :].reshape(8, 128, 1024))
    bmod_l = np.ascontiguousarray(b_mod[0].reshape(24, 128).T)
    gain_l = np.ascontiguousarray(norm_gain[0].reshape(8, 128).T)
    gn_l = np.ascontiguousarray(np.stack([np.tile(v[0], 2) for v in (qn_a, kn_a, qn_b, kn_b)], axis=1))
    sk = sink_b[0]
    sink_l = np.concatenate([np.concatenate([np.repeat(sk[j], 64), np.repeat(sk[4 + j], 64)]) for j in range(4)])[None, :]
    sink_l = np.ascontiguousarray(sink_l, dtype=np.float32)

    knbc_l = np.ascontiguousarray(np.broadcast_to(np.stack([kn_a[0], kn_b[0]], axis=0)[None], (128, 2, 64)), dtype=np.float32)

    in_maps = []
    for b in range(NCORES):
        cond_l = np.ascontiguousarray(np.stack([c[b].reshape(8, 128).T, c_ctx.reshape(8, 128).T], axis=2))
        d = {
            "xs": x_sample[b],
            "xp": np.ascontiguousarray(x_prompt[4 * b:4 * b + 4].reshape(NP, D)),
            "cka": np.ascontiguousarray(cache_k_a[b, 0].reshape(256, 128)),
            "cva": np.ascontiguousarray(cache_v_a[b, 0].reshape(256, 128)),
            "ckb": np.ascontiguousarray(cache_k_b[b, 0].reshape(256, 128)),
            "cvb": np.ascontiguousarray(cache_v_b[b, 0].reshape(256, 128)),
            "cond": cond_l,
            "wmod": wmod_l,
            "bmod": bmod_l,
            "gain": gain_l,
            "win": win_l,
            "wout": wout_l,
            "gn": gn_l,
            "sink": sink_l,
            "knbc": knbc_l,
        }
        d.update(consts)
        in_maps.append(d)

    res = run_bass_kernel_spmd(nc, in_maps, core_ids=list(range(NCORES)))
    R = res.results
    y_sample = np.stack([R[b]["ys"] for b in range(NCORES)], axis=0).astype(np.float32)
    y_prompt = np.concatenate([R[b]["yp"].reshape(4, 256, D) for b in range(NCORES)], axis=0).astype(np.float32)

    def gk(name):
        return np.concatenate([R[b][name].reshape(4, 1, 256, 2, 64) for b in range(NCORES)], axis=0).astype(np.float32)

    return (y_prompt, y_sample, gk("nka"), gk("nva"), gk("nkb"), gk("nvb"))
```

```python
import numpy as np
import ml_dtypes
import concourse.bass as bass
import concourse.mybir as mybir
from concourse.bass_utils import run_bass_kernel_spmd

F32 = mybir.dt.float32
BF16 = mybir.dt.bfloat16
AF = mybir.ActivationFunctionType
ALU = mybir.AluOpType

NCORES = 8
D = 1024
NS = 2048
NP = 1024
EPS = 1e-6
NEGV = -30000.0
_ENV = {}
STAGE1 = int(_ENV.get("KSTAGE", "99"))
STAGE2 = int(_ENV.get("KSTAGE2", "99"))


class Buf:
    __slots__ = ("name", "w", "r", "excl")

    def __init__(self, name, excl=False):
        self.name = name
        self.w = None
        self.r = {}
        self.excl = excl


class Op:
    __slots__ = ("eng", "fn", "deps", "pos", "signal", "semval", "dma", "dsem", "dval")

    def __init__(self, eng, fn, dma):
        self.eng = eng
        self.fn = fn
        self.dma = dma
        self.deps = []
        self.pos = -1
        self.signal = False
        self.semval = 0
        self.dsem = None
        self.dval = 0


class Prog:
    ENGS = ("pe", "act", "dve", "pool", "sp")
    SAME_ENG_WINDOW = 6

    def __init__(self, nc):
        self.nc = nc
        self.ops = {e: [] for e in self.ENGS}
        self.dma_pool_size = {"sp": 24, "pool": 12, "act": 4, "dve": 2, "pe": 2}
        self.dma_ops = {e: [] for e in self.ENGS}
        self.nbuf = 0

    def buf(self, name=None, excl=False):
        self.nbuf += 1
        return Buf(name or f"b{self.nbuf}", excl)

    def bufs(self, n, name="b"):
        return [self.buf(f"{name}{i}") for i in range(n)]

    def op(self, eng, fn, reads=(), writes=(), dma=False):
        o = Op(eng, fn, dma)
        deps = {}
        for b in reads:
            if b.w is not None:
                deps[id(b.w)] = b.w
            if b.excl:
                for k_, d in b.r.items():
                    if k_ != eng:
                        deps[id(d)] = d
        for b in writes:
            if b.w is not None:
                deps[id(b.w)] = b.w
            for d in b.r.values():
                deps[id(d)] = d
        if dma:
            lst = self.dma_ops[eng]
            k = len(lst)
            PS = self.dma_pool_size[eng]
            if k >= PS:
                d = lst[k - PS]
                deps[id(d)] = d
            lst.append(o)
        o.deps = list(deps.values())
        o.pos = len(self.ops[eng])
        self.ops[eng].append(o)
        for b in reads:
            key = ("dma", id(o)) if dma else eng
            b.r[key] = o
        for b in writes:
            b.w = o
            b.r = {}
        return o

    def _need_wait(self, o, d):
        if d.dma:
            return True
        if d.eng != o.eng:
            return True
        if o.dma:
            return True
        if o.eng == "pe":
            return False
        return (o.pos - d.pos) <= self.SAME_ENG_WINDOW

    def prepare(self):
        nc = self.nc
        for e in self.ENGS:
            for o in self.ops[e]:
                for d in o.deps:
                    if self._need_wait(o, d) and not d.dma:
                        d.signal = True
        self.esem = {e: nc.alloc_semaphore(f"s_{e}") for e in self.ENGS}
        for e in self.ENGS:
            c = 0
            for o in self.ops[e]:
                if not o.dma and o.signal:
                    c += 1
                    o.semval = c
        for e in self.ENGS:
            n = len(self.dma_ops[e])
            if n == 0:
                continue
            PS = self.dma_pool_size[e]
            sems = [nc.alloc_semaphore(f"d_{e}{i}") for i in range(min(PS, n))]
            for k, o in enumerate(self.dma_ops[e]):
                o.dsem = sems[k % PS]
                o.dval = 16 * (k // PS + 1)

    def emit_engine(self, e, eng):
        seen = {}
        for o in self.ops[e]:
            waits = {}
            for d in o.deps:
                if not self._need_wait(o, d):
                    continue
                if d.dma:
                    key = ("d", id(d.dsem))
                    sem, val = d.dsem, d.dval
                else:
                    key = ("e", d.eng)
                    sem, val = self.esem[d.eng], d.semval
                if seen.get(key, 0) >= val:
                    continue
                if key not in waits or waits[key][1] < val:
                    waits[key] = (sem, val)
            for key, (sem, val) in waits.items():
                eng.wait_ge(sem, val)
                seen[key] = val
            inst = o.fn(eng)
            if o.dma:
                inst.then_inc(o.dsem, 16)
            elif o.signal:
                inst.then_inc(self.esem[e], 1)


class Ring:
    def __init__(self, items):
        self.items = items
        self.i = 0

    def next(self):
        it = self.items[self.i % len(self.items)]
        self.i += 1
        return it


def _constants():
    ident = np.eye(128, dtype=np.float32)
    bd = np.zeros((128, 128), np.float32)
    bd[:64, :64] = 1.0
    bd[64:, 64:] = 1.0
    rm = np.zeros((128, 128), np.float32)
    for blk in (0, 64):
        for i in range(64):
            sec = i // 16
            if sec in (0, 2):
                rm[blk + i + 16, blk + i] = -1.0
            else:
                rm[blk + i - 16, blk + i] = 1.0
    t = np.arange(NS)
    row = (t // 64).astype(np.float64)
    col = (t % 64).astype(np.float64)
    inv = 10000.0 ** (-np.arange(16, dtype=np.float64) / 16)
    ang = np.zeros((128, NS), np.float64)
    for p in range(128):
        d = p % 64
        f = d % 16
        ang[p] = (row if d < 32 else col) * inv[f]
    cos = np.cos(ang).astype(np.float32)
    sin = np.sin(ang).astype(np.float32)
    neg = np.zeros((128, 384), np.float32)
    jl = np.arange(128)[:, None]
    il = np.arange(128)[None, :]
    neg[:, :] = 1.0
    neg[:, 0:128] = np.where(jl <= il, 1.0, 0.0)
    neg[:, 256:384] = np.where(il <= jl, 1.0, 0.0)
    bf = ml_dtypes.bfloat16
    return {
        "c_identf": ident,
        "c_identb": ident.astype(bf),
        "c_bd": bd.astype(bf),
        "c_rm": rm.astype(bf),
        "c_cos": cos,
        "c_sin": sin,
        "c_neg": neg,
    }


def _pair_cols(base, j):
    return list(range(base + j * 64, base + (j + 1) * 64)) + list(range(base + (4 + j) * 64, base + (5 + j) * 64))


def _win_perm():
    cols = []
    for j in range(4):
        cols += _pair_cols(0, j)
    for j in range(4):
        cols += _pair_cols(768, j)
    for j in range(4):
        cols += _pair_cols(1280, j)
    for j in range(4):
        cols += _pair_cols(2048, j)
    cols += list(range(512, 640)) + list(range(1792, 1920)) + list(range(640, 768)) + list(range(1920, 2048))
    return np.array(cols)


def _wout_perm():
    rows = []
    for j in range(4):
        rows += _pair_cols(0, j)
    for j in range(4):
        rows += _pair_cols(512, j)
    return np.array(rows)


CH_Q = {"A": 0, "B": 8}
CH_G = {"A": 4, "B": 12}
CH_K = {"A": 16, "B": 17}
CH_V = {"A": 18, "B": 19}


def build_nc():
    nc = bass.Bass("TRN2", target_bir_lowering=False)
    try:
        nc.allow_low_precision("bf16 matmul operands, fp32 accumulation (reference tolerance is bf16-level)")
    except Exception:
        pass
    P = Prog(nc)

    def din(name, shape, dt=F32):
        return nc.dram_tensor(name, list(shape), dt, kind="ExternalInput").ap()

    def dout(name, shape):
        return nc.dram_tensor(name, list(shape), F32, kind="ExternalOutput").ap()

    xs_d = din("xs", [NS, D])
    xp_d = din("xp", [NP, D])
    cka_d = din("cka", [256, 128])
    cva_d = din("cva", [256, 128])
    ckb_d = din("ckb", [256, 128])
    cvb_d = din("cvb", [256, 128])
    cond_d = din("cond", [128, 8, 2])
    wmod_d = din("wmod", [24, 128, 8 * 128])
    bmod_d = din("bmod", [128, 24])
    gain_d = din("gain", [128, 8])
    win_d = din("win", [20, 128, 8 * 128])
    wout_d = din("wout", [8, 128, 1024])
    gn_d = din("gn", [128, 4])
    sink_d = din("sink", [1, 4 * 128])
    knbc_d = din("knbc", [128, 2, 64])
    identf_d = din("c_identf", [128, 128])
    identb_d = din("c_identb", [128, 128], BF16)
    bd_d = din("c_bd", [128, 128], BF16)
    rm_d = din("c_rm", [128, 128], BF16)
    cos_d = din("c_cos", [128, NS])
    sin_d = din("c_sin", [128, NS])
    neg_d = din("c_neg", [128, 384])

    ys_d = dout("ys", [NS, D])
    yp_d = dout("yp", [NP, D])
    nk_d = {"A": dout("nka", [NP, 128]), "B": dout("nkb", [NP, 128])}
    nv_d = {"A": dout("nva", [NP, 128]), "B": dout("nvb", [NP, 128])}

    def sb(name, shape, dt=F32):
        return nc.alloc_sbuf_tensor(name, list(shape), dt).ap()

    hT = sb("hT", [128, 8, NS], BF16)
    omT = sb("omT", [128, 8, NS], BF16)
    kT = {"A": sb("kTa", [128, 256 + NS], BF16), "B": sb("kTb", [128, 256 + NS], BF16)}
    Vsb = sb("Vsb", [128, 18, 256], BF16)
    qT = [sb(f"qT{i}", [128, NS], BF16) for i in range(2)]
    sg = [sb(f"sg{i}", [128, NS], BF16) for i in range(2)]
    cos_t = [sb(f"cos{i}", [128, 512], F32) for i in range(2)]
    sin_t = [sb(f"sin{i}", [128, 512], F32) for i in range(2)]
    PT = [sb(f"PT{i}", [128, 2, 512], BF16) for i in range(3)]
    xr = [sb(f"xr{i}", [128, 1024], F32) for i in range(2)]
    xsb = [sb(f"xsb{i}", [128, 1024], BF16) for i in range(2)]
    wst = [sb(f"wst{i}", [128, 1024], F32) for i in range(3)]
    wbf = [sb(f"wbf{i}", [128, 8, 128], BF16) for i in range(4)]
    wv = sb("wv", [128, 8, 512], BF16)
    gate_bc = [sb(f"gatebc{i}", [128, 1024], F32) for i in range(2)]
    sq_t = [sb(f"sq{i}", [128, 512], BF16) for i in range(2)]
    rstd_t = [sb(f"rstd{i}", [128, 512], F32) for i in range(2)]
    kh_t = [sb(f"kh{i}", [128, 512], BF16) for i in range(2)]
    t1_t = [sb(f"t1_{i}", [128, 512], F32) for i in range(2)]
    t2_t = [sb(f"t2_{i}", [128, 512], F32) for i in range(2)]
    xc_t = [sb(f"xc{i}", [128, 512], F32) for i in range(2)]
    rY_t = [sb(f"rY{i}", [128, 512], F32) for i in range(2)]
    yb_t = [sb(f"yb{i}", [128, 512], F32) for i in range(2)]
    kout_t = [sb(f"kout{i}", [128, 256], F32) for i in range(1)]
    knbc = sb("knbc_sb", [128, 2, 64], F32)
    kst = sb("kst", [128, 8, 12], F32)
    kjunk = sb("kjunk", [128, 64], BF16)
    vout_t = [sb(f"vout{i}", [128, 256], F32) for i in range(2)]
    cst_t = sb("cst", [128, 4, 2, 128], F32)
    cbf_t = sb("cbf", [128, 2, 2, 128], BF16)
    identf = sb("identf", [128, 128], F32)
    identb = sb("identb", [128, 128], BF16)
    bd_t = sb("bd", [128, 128], BF16)
    rm_t = sb("rm", [128, 128], BF16)
    neg_t = sb("neg", [128, 384], F32)
    ones_b = sb("onesb", [128, 512], BF16)
    ones_f = sb("onesf", [128, 128], F32)
    diag_t = [sb(f"diag{i}", [128, 128], F32) for i in range(2)]
    s_sb = sb("s_sb", [128, 8, 2], F32)
    se_sb = sb("se_sb", [128, 8, 2], F32)
    mod_sb = sb("mod_sb", [128, 24, 2], F32)
    gp_sb = sb("gp_sb", [128, 8, 2], F32)
    bmod_sb = sb("bmod_sb", [128, 24], F32)
    gain_sb = sb("gain_sb", [128, 8], F32)
    gn_sb = sb("gn_sb", [128, 4], F32)
    sink_sb = sb("sink_sb", [1, 512], F32)
    esink = sb("esink", [1, 512], BF16)
    eps_t = sb("eps_t", [128, 1], F32)
    stats = sb("stats", [128, 24, 4], F32)

    S2 = [nc.alloc_psum_tensor(f"S{i}", [128, 2, 512], F32).ap() for i in range(2)]
    S2b = [a.bitcast(BF16) for a in S2]
    bank = [S2[0][:, 0, :], S2[0][:, 1, :], S2[1][:, 0, :], S2[1][:, 1, :]]
    bank += [nc.alloc_psum_tensor(f"bank{i}", [128, 512], F32).ap() for i in range(4, 8)]
    bankb = [P.buf(f"bank{i}", excl=True) for i in range(8)]

    def bank_bf(i, shape):
        a = S2b[i // 2][:, i % 2, :] if i < 4 else bank[i].bitcast(BF16)
        if len(shape) == 3:
            return a[:, 0:shape[1] * shape[2]].rearrange("p (a b) -> p a b", a=shape[1])
        return a[:, 0:shape[1]]

    b_hT = P.bufs(16, "hT")
    b_omT = [[P.buf(f"om{j}_{t}") for t in range(4)] for j in range(8)]
    b_kT = {m: P.bufs(5, f"kT{m}") for m in "AB"}
    b_V = P.bufs(18, "V")
    b_q = [P.bufs(4, f"q{i}_") for i in range(2)]
    b_sg = [P.bufs(4, f"sg{i}_") for i in range(2)]
    b_const = P.buf("const")
    b_mod = P.buf("mod")
    b_gate = P.bufs(2, "gate")
    b_stats = P.bufs(24, "stats")
    b_wv = P.buf("wv")

    def rng(items):
        return Ring(items)

    b_cst = P.bufs(4, "cst")
    _cst2 = cst_t.rearrange("p a b c -> p (a b c)")
    pc_t = [_cst2[:, 0:512], _cst2[:, 512:1024]]
    pc_b = [[b_cst[0], b_cst[1]], [b_cst[2], b_cst[3]]]

    PT_r = rng(list(zip(PT, P.bufs(3, "PT"))))
    xr_r = rng(list(zip(xr, P.bufs(2, "xr"))))
    xsb_r = rng(list(zip(xsb, P.bufs(2, "xsb"))))
    wst_r = rng(list(zip(wst, P.bufs(3, "wst"))))
    wbf_r = rng(list(zip(wbf, P.bufs(4, "wbf"))))
    sq_r = rng(list(zip(sq_t, P.bufs(2, "sq"))))
    rstd_r = rng(list(zip(rstd_t, P.bufs(2, "rstd"))))
    kh_r = rng(list(zip(kh_t, P.bufs(2, "kh"))))
    t1_r = rng(list(zip(t1_t, P.bufs(2, "t1"))))
    t2_r = rng(list(zip(t2_t, P.bufs(2, "t2"))))
    xc_r = rng(list(zip(xc_t, P.bufs(2, "xc"))))
    rY_r = rng(list(zip(rY_t, P.bufs(2, "rY"))))
    _ybl = list(zip(yb_t, P.bufs(2, "yb")))
    yb_r = rng(_ybl)
    eg_r = rng(_ybl)
    kout_r = rng(list(zip(kout_t, P.bufs(1, "kout"))))
    vout_r = rng(list(zip(vout_t, P.bufs(2, "vout"))))
    diag_r = rng(list(zip(diag_t, P.bufs(2, "diag"))))
    b_cs = P.bufs(2, "cs")
    b_sn = P.bufs(2, "sn")
    out_dma_bufs = []

    def dma(out, in_, reads=(), writes=(), is_out=False, q="sp"):
        if is_out:
            b = P.buf()
            out_dma_bufs.append(b)
            writes = list(writes) + [b]
            q = "pool"
        return P.op(q, lambda e: e.dma_start(out=out, in_=in_), reads, writes, dma=True)

    def mm(out, lhsT, rhs, start, stop, reads, writes, tp=None):
        if tp is None:
            return P.op("pe", lambda e: e.matmul(out, lhsT=lhsT, rhs=rhs, start=start, stop=stop), reads, writes)
        return P.op("pe", lambda e: e.matmul(out, lhsT=lhsT, rhs=rhs, start=start, stop=stop, tile_position=tp), reads, writes)

    def tr(out, in_, ident, reads, writes):
        return P.op("pe", lambda e: e.transpose(out, in_, ident), reads, writes)

    def act(out, in_, func, reads, writes, bias=None, scale=1.0, accum_out=None):
        kw = {}
        if bias is not None:
            kw["bias"] = bias
        if accum_out is not None:
            kw["accum_out"] = accum_out
        return P.op("act", lambda e: e.activation(out=out, in_=in_, func=func, scale=scale, **kw), reads, writes)

    def ts(eng, out, in0, s1, s2, op0, op1, reads, writes):
        if s2 is None:
            return P.op(eng, lambda e: e.tensor_scalar(out=out, in0=in0, scalar1=s1, scalar2=None, op0=op0), reads, writes)
        return P.op(eng, lambda e: e.tensor_scalar(out=out, in0=in0, scalar1=s1, scalar2=s2, op0=op0, op1=op1), reads, writes)

    def tt(eng, out, in0, in1, op, reads, writes):
        return P.op(eng, lambda e: e.tensor_tensor(out=out, in0=in0, in1=in1, op=op), reads, writes)

    def stt(eng, out, in0, scalar, in1, op0, op1, reads, writes):
        return P.op(eng, lambda e: e.scalar_tensor_tensor(out=out, in0=in0, scalar=scalar, in1=in1, op0=op0, op1=op1), reads, writes)

    def cp(eng, out, in_, reads, writes):
        return P.op(eng, lambda e: e.tensor_copy(out=out, in_=in_), reads, writes)

    def recip(out, in_, reads, writes):
        return P.op("dve", lambda e: e.reciprocal(out=out, in_=in_), reads, writes)

    def memset(eng, ap, val, writes):
        return P.op(eng, lambda e: e.memset(ap, val), (), writes)

    b_c = {k: P.buf(k) for k in ("identf", "identb", "bd", "rm", "cos", "sin", "neg", "bmod", "gain", "gn", "sink", "s")}
    dma(identf, identf_d, writes=[b_c["identf"]])
    dma(identb, identb_d, writes=[b_c["identb"]])
    dma(s_sb, cond_d, writes=[b_c["s"]])
    dma(bmod_sb, bmod_d, writes=[b_c["bmod"]])
    dma(gain_sb, gain_d, writes=[b_c["gain"]])
    dma(gn_sb, gn_d, writes=[b_c["gn"]])
    dma(sink_sb, sink_d, writes=[b_c["sink"]])
    b_knbc = P.buf("knbc")
    dma(knbc, knbc_d, writes=[b_knbc])
    b_kst = P.buf("kst")
    memset("dve", kst, 0.0, [b_kst])
    b_ones = P.buf("ones")
    b_eps = P.buf("eps")
    memset("pool", ones_b, 1.0, [b_ones])
    memset("pool", ones_f, 1.0, [b_ones])
    memset("dve", eps_t, EPS, [b_eps])
    for i in range(24):
        pass
    memset("dve", stats, 0.0, b_stats)

    act(se_sb, s_sb, AF.Exp, [b_c["s"]], [b_mod], scale=-1.0)
    ts("dve", se_sb, se_sb, 1.0, None, ALU.add, None, [b_mod], [b_mod])
    recip(se_sb, se_sb, [b_mod], [b_mod])
    tt("dve", s_sb, s_sb, se_sb, ALU.mult, [b_mod, b_c["s"]], [b_c["s"]])

    mod_ps = bank[4][:, 0:48].rearrange("p (a b) -> p a b", a=24)
    s_bf = sb("s_bf", [128, 8, 2], BF16)
    cp("dve", s_bf, s_sb, [b_c["s"]], [b_c["s"]])
    mod_ring = rng(list(wst_r.items) + list(xr_r.items))
    for nch in range(24):
        wa, wb = mod_ring.next()
        dma(wa, wmod_d[nch], writes=[wb])
        ba_, bb_ = wbf_r.next()
        cp("dve", ba_.rearrange("p a b -> p (a b)"), wa, [wb], [bb_])
        for kc in range(8):
            mm(mod_ps[:, nch, :], ba_[:, kc, :], s_bf[:, kc, :], kc == 0, kc == 7, [bb_, b_c["s"]], [bankb[4]])
    dma(bd_t, bd_d, writes=[b_c["bd"]])
    dma(rm_t, rm_d, writes=[b_c["rm"]])
    dma(neg_t, neg_d, writes=[b_c["neg"]])
    for m in range(2):
        tt("dve", mod_sb[:, :, m], mod_ps[:, :, m], bmod_sb, ALU.add, [bankb[4], b_c["bmod"]], [b_mod])
    for m in range(2):
        stt("dve", gp_sb[:, :, m], mod_sb[:, 8:16, m], 1.0, gain_sb, ALU.add, ALU.mult, [b_mod, b_c["gain"]], [b_mod])
    for m in range(2):
        for c in range(8):
            da, db = diag_r.next()
            ts("dve", da, identf, mod_sb[:, 16 + c, m:m + 1], None, ALU.mult, None, [b_mod, b_c["identf"]], [db])
            bk = 5 + (c // 4)
            mm(bank[bk][:, (c % 4) * 128:(c % 4 + 1) * 128], ones_f, da, True, True, [b_ones, db], [bankb[bk]])
        for hlf in range(2):
            cp("dve", gate_bc[m][:, hlf * 512:(hlf + 1) * 512], bank[5 + hlf], [bankb[5 + hlf]], [b_gate[m]])
    act(esink, sink_sb, AF.Exp, [b_c["sink"]], [b_c["sink"]])

    stat_idx = [0]

    def preproc(xd, ntiles, m, tpb=(0, 1)):
        tp_banks = rng(list(tpb))

        def stage_a(t):
            xa, xb = xr_r.next()
            dma(xa, xd[t * 128:(t + 1) * 128, :], writes=[xb])
            xsa, xsbb = xsb_r.next()
            k = stat_idx[0]
            stat_idx[0] += 1
            bst = b_stats[k]
            act(xsa, xa, AF.Square, [xb], [xsbb, bst], accum_out=stats[:, k, 0:1])
            act(stats[:, k, 1:2], stats[:, k, 0:1], AF.Ln, [bst, b_eps], [bst], bias=eps_t, scale=1.0 / D)
            act(stats[:, k, 2:3], stats[:, k, 1:2], AF.Exp, [bst], [bst], scale=-0.5)
            act(xsa, xa, AF.Copy, [xb, bst], [xsbb], scale=stats[:, k, 2:3])
            return xsa, xsbb

        def stage_b(t, xsa, xsbb):
            bi = tp_banks.next()
            tpa = bank_bf(bi, [128, 8, 128])
            for c in range(8):
                tr(tpa[:, c, :], xsa[:, c * 128:(c + 1) * 128], identb, [xsbb, b_c["identb"]], [bankb[bi]])
            for c in range(8):
                ts("dve", hT[:, c, t * 128:(t + 1) * 128], tpa[:, c, :], gp_sb[:, c, m:m + 1], mod_sb[:, c, m:m + 1],
                   ALU.mult, ALU.add, [bankb[bi], b_mod], [b_hT[t]])

        nxt = stage_a(0)
        for t in range(ntiles):
            cur = nxt
            if t + 1 < ntiles:
                nxt = stage_a(t + 1)
            stage_b(t, *cur)
            yield

    def load_w(ch):
        wa, wb = wst_r.next()
        dma(wa, win_d[ch], writes=[wb])
        ba, bb = wbf_r.next()
        cp("dve", ba.rearrange("p a b -> p (a b)"), wa, [wb], [bb])
        return ba, bb

    def load_wv(with_k):
        if _ENV.get("KDBG", "") == "h":
            with_k = False
        chs = [CH_V["A"], CH_V["B"]] + ([CH_K["A"], CH_K["B"]] if with_k else [])
        for i, ch in enumerate(chs):
            wa, wb = wst_r.next()
            dma(wa, win_d[ch], writes=[wb])
            cp("dve", wv[:, :, i * 128:(i + 1) * 128], wa.rearrange("p (a b) -> p a b", a=8), [wb], [b_wv])

    PB, SRB = 6, 7

    def proj_tile_g(w, tc, kind, gncol=None, rope=False, dst=None, dst_b=None, tok0=0, PB=6, SRB=7, slot=0, act_recip=False):
        wa, wb = w
        pa = bank[PB]
        toks = slice(tc * 512, (tc + 1) * 512)
        hb = b_hT[tc * 4:(tc + 1) * 4]
        if rope:
            dma(cos_t[slot], cos_d[:, tok0:tok0 + 512], writes=[b_cs[slot]], q="pool")
            dma(sin_t[slot], sin_d[:, tok0:tok0 + 512], writes=[b_sn[slot]], q="pool")
        for kc in range(8):
            mm(pa, wa[:, kc, :], hT[:, kc, toks], kc == 0, kc == 7, [wb] + hb, [bankb[PB]])
            if kc % 2 == 1:
                yield
        pc = pc_t[slot]
        pcb = pc_b[slot]
        cp("dve", pc, pa, [bankb[PB]], pcb)
        if kind == "g":
            yield
            ea, eb = _ybl[slot]
            act(ea, pc, AF.Exp, pcb, [eb], scale=-1.0)
            yield
            if act_recip:
                act(ea, ea, AF.Ln, [eb, b_ones], [eb], bias=ones_f[:, 0:1], scale=1.0)
                act(ea, ea, AF.Exp, [eb], [eb], scale=-1.0)
                yield
            else:
                ts("dve", ea, ea, 1.0, None, ALU.add, None, [eb], [eb])
                recip(ea, ea, [eb], [eb])
            tt("dve", dst, pc, ea, ALU.mult, pcb + [eb], dst_b)
            yield
            return
        sa, sbb = sq_r.items[slot]
        tt("dve", sa, pc, pc, ALU.mult, pcb, [sbb])
        yield
        mm(bank[SRB], bd_t, sa, True, True, [b_c["bd"], sbb], [bankb[SRB]])
        yield
        ra, rb = rstd_r.items[slot]
        act(ra, bank[SRB], AF.Ln, [bankb[SRB], b_eps], [rb], bias=eps_t, scale=1.0 / 64)
        act(ra, ra, AF.Exp, [rb], [rb], scale=-0.5)
        yield
        gcol = gn_sb[:, gncol:gncol + 1]
        if rope:
            ka, kb = kh_r.items[slot]
            stt("dve", ka, pc, gcol, ra, ALU.mult, ALU.mult, pcb + [rb, b_c["gn"]], [kb])
            yield
            mm(bank[SRB], rm_t, ka, True, True, [b_c["rm"], kb], [bankb[SRB]])
            yield
            t1a, t1b = t1_r.items[slot]
            t2a, t2b = t2_r.items[slot]
            tsl = slice(tok0, tok0 + 512)
            tt("dve", t1a, ka, cos_t[slot], ALU.mult, [kb, b_cs[slot]], [t1b])
            tt("dve", t2a, bank[SRB], sin_t[slot], ALU.mult, [bankb[SRB], b_sn[slot]], [t2b])
            tt("dve", dst, t1a, t2a, ALU.add, [t1b, t2b], dst_b)
            yield
        else:
            stt("dve", dst, pc, gcol, ra, ALU.mult, ALU.mult, pcb + [rb, b_c["gn"]], dst_b)
            yield

    def v_tiles(ntiles, vt_of, prompt):
        vb_r = rng([2, 3])
        ncol = 512 if (prompt and _ENV.get("KDBG", "") != "h") else 256
        for t in range(ntiles):
            bi = vb_r.next()
            va = bank[bi][:, 0:ncol]
            for kc in range(8):
                mm(va, hT[:, kc, t * 128:(t + 1) * 128], wv[:, kc, 0:ncol], kc == 0, kc == 7, [b_hT[t], b_wv], [bankb[bi]])
            vt = vt_of(t)
            P.op("act", lambda e, o=Vsb[:, vt, :], i=va[:, 0:256]: e.copy(out=o, in_=i), [bankb[bi]], [b_V[vt]])
            if prompt:
                voa, vob = vout_r.next()
                if not _ENV.get("KNOVCP"):
                    cp("dve", voa, va[:, 0:256], [bankb[bi]] + ([b_V[vt]] if _ENV.get("KSERIAL") else []), [vob])
                if not _ENV.get("KNOVDMA"):
                    dma(nv_d["A"][t * 128:(t + 1) * 128, :], voa[:, 0:128], reads=[vob], is_out=True)
                    dma(nv_d["B"][t * 128:(t + 1) * 128, :], voa[:, 128:256], reads=[vob], is_out=True)
                if _ENV.get("KDBG", "") in ("g", "h"):
                    yield
                    continue
                for hh in range(4):
                    act(kjunk, va[:, 256 + hh * 64:256 + (hh + 1) * 64], AF.Square, [bankb[bi]], [b_kst], accum_out=kst[:, t, hh:hh + 1])
                act(kst[:, t, 4:8], kst[:, t, 0:4], AF.Ln, [b_kst, b_eps], [b_kst], bias=eps_t, scale=1.0 / 64)
                act(kst[:, t, 8:12], kst[:, t, 4:8], AF.Exp, [b_kst], [b_kst], scale=-0.5)
                koa, kob = kout_r.next()
                for hh in range(4):
                    stt("dve", koa[:, hh * 64:(hh + 1) * 64], va[:, 256 + hh * 64:256 + (hh + 1) * 64], kst[:, t, 8 + hh:9 + hh],
                        knbc[:, hh // 2, :], ALU.mult, ALU.mult, [bankb[bi], b_kst, b_knbc], [kob])
                dma(nk_d["A"][t * 128:(t + 1) * 128, :], koa[:, 0:128], reads=[kob], is_out=True)
                dma(nk_d["B"][t * 128:(t + 1) * 128, :], koa[:, 128:256], reads=[kob], is_out=True)
            yield

    SB_ = [(0, 1), (2, 3)]
    XB, YB = 4, 5

    class Chunk:
        pass

    def make_chunk(mx, j, qa, qcol0, n, tiles, sga, sg_b, om_dst, om_b, q_b, sink=None):
        c = Chunk()
        c.sink = sink
        c.mx, c.j, c.qa, c.qcol0, c.n, c.tiles = mx, j, qa, qcol0, n, tiles
        c.sga, c.sg_b, c.om_dst, c.om_b, c.q_b = sga, sg_b, om_dst, om_b, q_b
        return c

    def att_stream(chunks, sink, s_slots=(0, 1), act_recip=False):
        s_ring = rng(list(s_slots))
        flat = [(c, i) for c in chunks for i in range(len(c.tiles))]
        sbuf_of = {}
        pt_of = {}

        def qk(t):
            c, i = flat[t]
            kT_ap, k_b, V_ap, v_b, q0, n, negsl = c.tiles[i]
            si = s_ring.next()
            sbuf_of[t] = si
            b0, b1 = SB_[si]
            qcols = slice(c.qcol0 + q0, c.qcol0 + q0 + n)
            for h, bk in ((0, b0), (1, b1)):
                rows = slice(h * 64, (h + 1) * 64)
                mm(bank[bk][:, 0:n], kT_ap[rows, :], c.qa[rows, qcols], True, True, list(k_b) + c.q_b, [bankb[bk]])

        def ex(t):
            c, i = flat[t]
            n = c.tiles[i][5]
            b0, b1 = SB_[sbuf_of[t]]
            pa, pb = PT_r.next()
            pt_of[t] = (pa, pb)
            src = S2[sbuf_of[t]][:, :, 0:n]
            act(pa[:, :, 0:n], src, AF.Exp, [bankb[b0], bankb[b1]], [pb], scale=0.125)
            negsl = c.tiles[i][6]
            if negsl is not None:
                for h in range(2):
                    tt("dve", pa[:, h, 0:n], pa[:, h, 0:n], neg_t[:, negsl], ALU.mult, [pb, b_c["neg"]], [pb])

        def pv(t):
            c, i = flat[t]
            kT_ap, k_b, V_ap, v_b, q0, n, negsl = c.tiles[i]
            pa, pb = pt_of[t]
            first = i == 0
            last = i == len(c.tiles) - 1
            cols = slice(q0, q0 + n)
            csink = sink if c.sink is None else c.sink
            if first and csink:
                mm(bank[YB][:, 0:c.n], esink[0:1, c.j * 128:(c.j + 1) * 128], ones_b[0:1, 0:c.n], True, False,
                   [b_c["sink"], b_ones], [bankb[YB]])
            for h in range(2):
                rows = slice(h * 64, (h + 1) * 64)
                mm(bank[YB][rows, cols], ones_b[:, rows], pa[:, h, 0:n], first and not csink, last, [b_ones, pb], [bankb[YB]],
                   tp=(0, h * 64))
            for h in range(2):
                rows = slice(h * 64, (h + 1) * 64)
                mm(bank[XB][rows, cols], V_ap[:, rows], pa[:, h, 0:n], first, last, list(v_b) + [pb], [bankb[XB]], tp=(0, h * 64))
            if last:
                n_ = c.n
                ra, rb = rY_r.next()
                xa_, xb_ = xc_r.next()
                if act_recip:
                    act(ra[:, 0:n_], bank[YB][:, 0:n_], AF.Ln, [bankb[YB]], [rb])
                    cp("dve", xa_[:, 0:n_], bank[XB][:, 0:n_], [bankb[XB]], [xb_])
                    act(ra[:, 0:n_], ra[:, 0:n_], AF.Exp, [rb], [rb], scale=-1.0)
                else:
                    cp("dve", ra[:, 0:n_], bank[YB][:, 0:n_], [bankb[YB]], [rb])
                    cp("dve", xa_[:, 0:n_], bank[XB][:, 0:n_], [bankb[XB]], [xb_])
                    recip(ra[:, 0:n_], ra[:, 0:n_], [rb], [rb])
                tt("dve", ra[:, 0:n_], ra[:, 0:n_], c.sga, ALU.mult, [rb] + c.sg_b, [rb])
                tt("dve", c.om_dst, xa_[:, 0:n_], ra[:, 0:n_], ALU.mult, [xb_, rb], c.om_b)

        T = len(flat)
        L = len(s_slots)
        for t in range(min(L, T)):
            qk(t)
        for t in range(T):
            ex(t)
            if L == 1:
                if t + 1 < T:
                    qk(t + 1)
                pv(t)
            else:
                pv(t)
                if t + L < T:
                    qk(t + L)
            yield

    def outproj(ntiles, xd, yd, m, wout_b, obanks=(0, 1, 2, 3)):
        ob_r = rng(list(obanks))
        for t in range(ntiles):
            xa, xb = wst_r.next()
            dma(xa, xd[t * 128:(t + 1) * 128, :], writes=[xb])
            for hlf in range(2):
                bi = ob_r.next()
                cols = slice(hlf * 512, (hlf + 1) * 512)
                for j in range(8):
                    mm(bank[bi], omT[:, j, t * 128:(t + 1) * 128], wout_ap(j)[:, cols], j == 0, j == 7,
                       [b_omT[j][t // 4]] + list(wout_b[j]), [bankb[bi]])
                ya, yb = yb_r.next()
                tt("dve", ya, bank[bi], gate_bc[m][:, cols], ALU.mult, [bankb[bi], b_gate[m]], [yb])
                tt("dve", xa[:, cols], xa[:, cols], ya, ALU.add, [xb, yb], [xb])
            dma(yd[t * 128:(t + 1) * 128, :], xa, reads=[xb], is_out=True)
            yield

    def wout_ap(j):
        return hT[:, j, 1024:2048]

    def load_wout():
        bl = []
        for j in range(8):
            wa, wb = wst_r.next()
            dma(wa, wout_d[j], writes=[wb])
            tiles = b_hT[(j % 2) * 8:(j % 2 + 1) * 8]
            cp("dve", wout_ap(j), wa, [wb], tiles)
            bl.append(tiles)
        return bl

    def run(gen):
        for _ in gen:
            pass

    b_cbf = P.bufs(2, "cbf")

    def ctx_prep():
        for idx, d in enumerate((cka_d, cva_d, ckb_d, cvb_d)):
            dma(cst_t[:, idx, :, :], d.rearrange("(i p) f -> p i f", p=128), writes=[b_cst[idx]])
        for mi, mx in enumerate("AB"):
            cp("dve", cbf_t[:, mi, :, :], cst_t[:, 2 * mi, :, :], [b_cst[2 * mi]], [b_cbf[mi]])
            for i in range(2):
                pt = bank_bf(i, [128, 128])
                tr(pt, cbf_t[:, mi, i, :], identb, [b_cbf[mi], b_c["identb"]], [bankb[i]])
                cp("dve", kT[mx][:, i * 128:(i + 1) * 128], pt, [bankb[i]], [b_kT[mx][0]])
            for i in range(2):
                cp("dve", Vsb[:, i, mi * 128:(mi + 1) * 128], cst_t[:, 2 * mi + 1, i, :], [b_cst[2 * mi + 1]], [b_V[i]])

    def sample_chunks(mx, j, buf):
        mcol = 0 if mx == "A" else 128
        jidx = (0 if mx == "A" else 4) + j
        chunks = []
        for c in range(4):
            tiles = []
            for i in range(2):
                tiles.append((kT[mx][:, i * 128:(i + 1) * 128], [b_kT[mx][0]], Vsb[:, i, mcol:mcol + 128], [b_V[i]], 0, 512, None))
            if mx == "A":
                kts = range(16)
            else:
                kts = range(max(0, 4 * c - 1), min(15, 4 * c + 4) + 1)
            for kt in kts:
                kap = kT[mx][:, 256 + kt * 128:256 + (kt + 1) * 128]
                kb = [b_kT[mx][1 + kt // 4]]
                vap = Vsb[:, 2 + kt, mcol:mcol + 128]
                vb = [b_V[2 + kt]]
                if mx == "A":
                    tiles.append((kap, kb, vap, vb, 0, 512, None))
                else:
                    qbs = [qb for qb in (kt - 1, kt, kt + 1) if 4 * c <= qb <= 4 * c + 3]
                    q0 = (qbs[0] - 4 * c) * 128
                    n = 128 * len(qbs)
                    rels = [qb - kt for qb in qbs]
                    negsl = slice((rels[0] + 1) * 128, (rels[-1] + 2) * 128)
                    if all(r == 0 for r in rels):
                        negsl = None
                    tiles.append((kap, kb, vap, vb, q0, n, negsl))
            chunks.append(make_chunk(mx, j, qT[buf], c * 512, 512, tiles, sg[buf][:, c * 512:(c + 1) * 512], [b_sg[buf][c]],
                                     omT[:, jidx, c * 512:(c + 1) * 512], [b_omT[jidx][c]], [b_q[buf][c]]))
        return chunks

    def prompt_chunks(mx, j, buf, colbase=0, tokbase=0):
        mcol = 0 if mx == "A" else 128
        jidx = (0 if mx == "A" else 4) + j
        chunks = []
        for pb in range(4):
            tiles = []
            for i in range(2):
                t = pb * 2 + i
                tiles.append((kT[mx][:, t * 128:(t + 1) * 128], b_kT[mx], Vsb[:, t, mcol:mcol + 128], [b_V[t]], 0, 256, None))
            c0 = colbase + pb * 256
            chunks.append(make_chunk(mx, j, qT[buf], c0, 256, tiles, sg[buf][:, c0:c0 + 256], [b_sg[buf][tokbase + pb // 2]],
                                     omT[:, jidx, pb * 256:(pb + 1) * 256], [b_omT[jidx][pb // 2]], [b_q[buf][tokbase + pb // 2]],
                                     sink=(mx == "B")))
        return chunks

    pair_ctr = [0]

    def mix(fg, bg, ratio):
        acc = 0.0
        alive = bg is not None
        for _ in fg:
            acc += 1.0 / ratio
            while alive and acc >= 1.0:
                acc -= 1.0
                try:
                    next(bg)
                except StopIteration:
                    alive = False
        if alive:
            for _ in bg:
                pass

    def chains_g(specs, bank_pairs):
        pending = list(specs)
        active = [None] * len(bank_pairs)
        while pending or any(a is not None for a in active):
            for i, bp in enumerate(bank_pairs):
                if active[i] is None and pending:
                    active[i] = pending.pop(0)(bp[0], bp[1], i)
                if active[i] is not None:
                    try:
                        next(active[i])
                    except StopIteration:
                        active[i] = None
                        continue
                    yield

    def pair_specs(sample, mx, j, buf, ntc, wq, wg, act_recip):
        specs = []
        for tc in range(ntc):
            specs.append(lambda pb, srb, slot, tc=tc: proj_tile_g(wq, tc, "q", gncol=(0 if mx == "A" else 2), rope=sample,
                                                            dst=qT[buf][:, tc * 512:(tc + 1) * 512], dst_b=[b_q[buf][tc]],
                                                            tok0=tc * 512, PB=pb, SRB=srb, slot=slot))
            specs.append(lambda pb, srb, slot, tc=tc: proj_tile_g(wg, tc, "g", dst=sg[buf][:, tc * 512:(tc + 1) * 512],
                                                            dst_b=[b_sg[buf][tc]], PB=pb, SRB=srb, slot=slot, act_recip=act_recip))
        return specs

    def pair_proj_g(sample, mx, j, buf, ntc, bank_pairs, act_recip):
        wq = load_w(CH_Q[mx] + j)
        wg = load_w(CH_G[mx] + j)
        yield
        yield from chains_g(pair_specs(sample, mx, j, buf, ntc, wq, wg, act_recip), bank_pairs)

    def k_specs(sample, wk, ntc):
        specs = []
        for tc in range(ntc):
            for mx in "AB":
                gcol = 1 if mx == "A" else 3
                if sample:
                    specs.append(lambda pb, srb, slot, tc=tc, mx=mx, gcol=gcol: proj_tile_g(
                        wk[mx], tc, "k", gncol=gcol, rope=True, dst=kT[mx][:, 256 + tc * 512:256 + (tc + 1) * 512],
                        dst_b=[b_kT[mx][1 + tc]], tok0=tc * 512, PB=pb, SRB=srb, slot=slot))
                else:
                    specs.append(lambda pb, srb, slot, tc=tc, mx=mx, gcol=gcol: proj_tile_g(
                        wk[mx], tc, "k", gncol=gcol, rope=False, dst=kT[mx][:, tc * 512:(tc + 1) * 512],
                        dst_b=b_kT[mx], PB=pb, SRB=srb, slot=slot))
        return specs

    def wout_g(holder):
        bl = []
        for j in range(8):
            wa, wb = wst_r.next()
            dma(wa, wout_d[j], writes=[wb])
            tiles = b_hT[8:16]
            cp("dve", wout_ap(j), wa, [wb], tiles)
            bl.append(tiles)
            yield
        holder.extend(bl)

    def mix_g(fg, bg, ratio):
        acc = 0.0
        alive = bg is not None
        for _ in fg:
            acc += 1.0 / ratio
            while alive and acc >= 1.0:
                acc -= 1.0
                try:
                    next(bg)
                except StopIteration:
                    alive = False
            yield
        if alive:
            for _ in bg:
                yield

    def take(gens, n):
        for _ in range(n):
            for g in gens:
                try:
                    next(g)
                except StopIteration:
                    pass
            yield

    def front_g(sample, tpb, extra_specs=None):
        m = 0 if sample else 1
        xd, ntiles = (xs_d, 16) if sample else (xp_d, 8)
        ntc = ntiles // 4
        wk = {"A": load_w(CH_K["A"]), "B": load_w(CH_K["B"])}
        load_wv(not sample)
        pre = preproc(xd, ntiles, m, tpb)
        vgen = v_tiles(ntiles, (lambda t: 2 + t) if sample else (lambda t: t), not sample)
        ksp = k_specs(sample, wk, ntc)
        esp = extra_specs() if extra_specs is not None else []
        kbanks = [(6, 6), (7, 7)]

        def sp(tc):
            out = ksp[2 * tc:2 * tc + 2]
            if esp:
                out = out + esp[2 * tc:2 * tc + 2]
            return out

        nst = 2 * (10 if sample else 8) + (18 if esp else 0)
        yield from take([pre], 4)
        if sample:
            ctx_prep()
        for g in range(1, ntc):
            yield from mix_g(take([pre, vgen], 4), chains_g(sp(g - 1), kbanks), 4.0 / nst)
        yield from mix_g(take([vgen], 4), chains_g(sp(ntc - 1), kbanks), 4.0 / nst)

    def run_all():
        ntc = 4
        pairs = [(mx, j) for mx in "AB" for j in range(4)]
        bufs = []
        for _ in pairs:
            bufs.append(pair_ctr[0] % 2)
            pair_ctr[0] += 1

        def first_specs():
            wq = load_w(CH_Q["A"] + 0)
            wg = load_w(CH_G["A"] + 0)
            return pair_specs(True, "A", 0, bufs[0], ntc, wq, wg, True)

        run(front_g(True, (0, 1), first_specs))
        nbg = 1 + ntc * (10 + 8)
        wout_hold = []
        for idx, (mx, j) in enumerate(pairs):
            chunks = sample_chunks(mx, j, bufs[idx])
            nfg = sum(len(c.tiles) for c in chunks)
            wide = mx == "A"
            bps = [(6, 6), (7, 7)]
            if idx + 1 < len(pairs):
                bg = pair_proj_g(True, pairs[idx + 1][0], pairs[idx + 1][1], bufs[idx + 1], ntc, bps, not wide)
                ratio = max(nfg / nbg, 1.0 / (1.5 * len(bps)))
            else:
                bg = wout_g(wout_hold)
                ratio = nfg / 8
            mix(att_stream(chunks, sink=(mx == "B"), s_slots=(0, 1), act_recip=not wide), bg, ratio)
        kb2 = [(6, 6), (7, 7)]

        def unit_proj_g(j, buf):
            w = {}
            for mx in "AB":
                w[(mx, "q")] = load_w(CH_Q[mx] + j)
                w[(mx, "g")] = load_w(CH_G[mx] + j)
            yield
            specs = []
            for tc in range(2):
                for mi, mx in enumerate("AB"):
                    c0 = mi * 1024 + tc * 512
                    specs.append(lambda pb, srb, slot, tc=tc, mi=mi, mx=mx, c0=c0: proj_tile_g(
                        w[(mx, "q")], tc, "q", gncol=(0 if mx == "A" else 2), rope=False, dst=qT[buf][:, c0:c0 + 512],
                        dst_b=[b_q[buf][mi * 2 + tc]], PB=pb, SRB=srb, slot=slot))
                    specs.append(lambda pb, srb, slot, tc=tc, mi=mi, mx=mx, c0=c0: proj_tile_g(
                        w[(mx, "g")], tc, "g", dst=sg[buf][:, c0:c0 + 512], dst_b=[b_sg[buf][mi * 2 + tc]],
                        PB=pb, SRB=srb, slot=slot, act_recip=True))
            yield from chains_g(specs, kb2)

        ubuf = []
        for _ in range(4):
            ubuf.append(pair_ctr[0] % 2)
            pair_ctr[0] += 1

        def seq_g(*gens):
            for g in gens:
                yield from g

        mix(outproj(16, xs_d, ys_d, 0, wout_hold, obanks=(0, 1)),
            seq_g(front_g(False, (4, 5)), unit_proj_g(0, ubuf[0])), 16.0 / 126.0)

        for u in range(4):
            ca = prompt_chunks("A", u, ubuf[u], 0, 0)
            cb = prompt_chunks("B", u, ubuf[u], 1024, 2)
            chunks = [c for pr in zip(ca, cb) for c in pr]
            nfg = sum(len(c.tiles) for c in chunks)
            bg = unit_proj_g(u + 1, ubuf[u + 1]) if u + 1 < 4 else None
            mix(att_stream(chunks, sink=False, s_slots=(0, 1), act_recip=True), bg, max(nfg / 65.0, 1.0 / 3.0))
        run(outproj(8, xp_d, yp_d, 1, wout_hold))

    run_all()
    for _i in range(int(_ENV.get("KDUMMY", "0"))):
        mm(bank[6][:, 0:128], identb, identb, True, True, [b_c["identb"]], [bankb[6]])
    P.op("sp", lambda e: e.nop(), reads=out_dma_bufs)

    P.prepare()
    with nc.Block() as block:

        @block.tensor
        def _(e):
            P.emit_engine("pe", e)

        @block.scalar
        def _(e):
            P.emit_engine("act", e)

        @block.vector
        def _(e):
            P.emit_engine("dve", e)

        @block.gpsimd
        def _(e):
            P.emit_engine("pool", e)

        @block.sync
        def _(e):
            P.emit_engine("sp", e)

    return nc, P


_CACHE = {}


def kernel(x_prompt, x_sample, cache_k_a, cache_v_a, cache_k_b, cache_v_b, c, c_ctx,
           w_mod, b_mod, norm_gain, w_in, qn_a, kn_a, qn_b, kn_b, sink_b, w_out):
    f = lambda a: np.ascontiguousarray(np.asarray(a), dtype=np.float32)
    x_prompt, x_sample = f(x_prompt), f(x_sample)
    cache_k_a, cache_v_a, cache_k_b, cache_v_b = f(cache_k_a), f(cache_v_a), f(cache_k_b), f(cache_v_b)
    c, c_ctx, w_mod, b_mod, norm_gain, w_in = f(c), f(c_ctx), f(w_mod), f(b_mod), f(norm_gain), f(w_in)
    qn_a, kn_a, qn_b, kn_b, sink_b, w_out = f(qn_a), f(kn_a), f(qn_b), f(kn_b), f(sink_b), f(w_out)

    if "nc" not in _CACHE:
        _CACHE["nc"] = build_nc()[0]
        _CACHE["consts"] = _constants()
    nc = _CACHE["nc"]
    consts = _CACHE["consts"]

    wmod_l = np.ascontiguousarray(w_mod[0].reshape(8, 128, 24, 128).transpose(2, 1, 0, 3)).reshape(24, 128, 1024)
    win_p = w_in[0][:, _win_perm()]
    win_l = np.ascontiguousarray(win_p.reshape(8, 128, 20, 128).transpose(2, 1, 0, 3)).reshape(20, 128, 1024)
    wout_l = np.ascontiguousarray(w_out[0][_wout_perm(), :].reshape(8, 128, 1024))
    bmod_l = np.ascontiguousarray(b_mod[0].reshape(24, 128).T)
    gain_l = np.ascontiguousarray(norm_gain[0].reshape(8, 128).T)
    gn_l = np.ascontiguousarray(np.stack([np.tile(v[0], 2) for v in (qn_a, kn_a, qn_b, kn_b)], axis=1))
    sk = sink_b[0]
    sink_l = np.concatenate([np.concatenate([np.repeat(sk[j], 64), np.repeat(sk[4 + j], 64)]) for j in range(4)])[None, :]
    sink_l = np.ascontiguousarray(sink_l, dtype=np.float32)

    knbc_l = np.ascontiguousarray(np.broadcast_to(np.stack([kn_a[0], kn_b[0]], axis=0)[None], (128, 2, 64)), dtype=np.float32)

    in_maps = []
    for b in range(NCORES):
        cond_l = np.ascontiguousarray(np.stack([c[b].reshape(8, 128).T, c_ctx.reshape(8, 128).T], axis=2))
        d = {
            "xs": x_sample[b],
            "xp": np.ascontiguousarray(x_prompt[4 * b:4 * b + 4].reshape(NP, D)),
            "cka": np.ascontiguousarray(cache_k_a[b, 0].reshape(256, 128)),
            "cva": np.ascontiguousarray(cache_v_a[b, 0].reshape(256, 128)),
            "ckb": np.ascontiguousarray(cache_k_b[b, 0].reshape(256, 128)),
            "cvb": np.ascontiguousarray(cache_v_b[b, 0].reshape(256, 128)),
            "cond": cond_l,
            "wmod": wmod_l,
            "bmod": bmod_l,
            "gain": gain_l,
            "win": win_l,
            "wout": wout_l,
            "gn": gn_l,
            "sink": sink_l,
            "knbc": knbc_l,
        }
        d.update(consts)
        in_maps.append(d)

    res = run_bass_kernel_spmd(nc, in_maps, core_ids=list(range(NCORES)))
    R = res.results
    y_sample = np.stack([R[b]["ys"] for b in range(NCORES)], axis=0).astype(np.float32)
    y_prompt = np.concatenate([R[b]["yp"].reshape(4, 256, D) for b in range(NCORES)], axis=0).astype(np.float32)

    def gk(name):
        return np.concatenate([R[b][name].reshape(4, 1, 256, 2, 64) for b in range(NCORES)], axis=0).astype(np.float32)

    return (y_prompt, y_sample, gk("nka"), gk("nva"), gk("nkb"), gk("nvb"))
```
